# Optimizing a Trainium2 kernel written in Bass

```python
import math
import jax
import jax.numpy as jnp
from jax import lax
import numpy as np

D_MODEL = 2048
BATCH = 4
SEQ = 8192
DEPTH = 1

A_HEADS = 4
A_QK_DIM = 256
A_V_DIM = 512
A_CONV = 4
CHUNK = 128
B_HEADS = 8
B_HEAD_DIM = 128
Q_BLOCK = 128
D_FF = 5632
FFN_CONV = 3
EPS = 1e-6

A_QK = A_HEADS * A_QK_DIM
A_V = A_HEADS * A_V_DIM
B_QK = B_HEADS * 2 * B_HEAD_DIM
B_V = B_HEADS * 2 * B_HEAD_DIM
SPLIT_SIZES = (2 * A_QK, A_V, A_V, 2 * A_HEADS, B_QK, B_QK, B_V, D_MODEL, D_MODEL)
P_IN = sum(SPLIT_SIZES)

kernel_name = 'hybrid_mlstm_diffattn_convffn'


def rmsnorm(x, g):
    xf = x.astype(jnp.float32)
    y = xf * lax.rsqrt(jnp.mean(xf * xf, axis=-1, keepdims=True) + EPS)
    return (y * g.astype(jnp.float32)).astype(x.dtype)


def causal_dwconv(x, w, b):
    k_w = w.shape[0]
    s = x.shape[1]
    xp = jnp.pad(x, ((0, 0), (k_w - 1, 0), (0, 0)))
    y = b + xp[:, 0:s, :] * w[0]
    for j in range(1, k_w):
        y = y + xp[:, j:j + s, :] * w[j]
    return y


def alibi_slopes(n):
    return jnp.asarray(2.0 ** (-8.0 * np.arange(1, n + 1) / n), dtype=jnp.float32)


def mlstm_chunkwise(q, k, v, i_pre, f_pre):
    bsz, nh, s, dk = q.shape
    dv = v.shape[-1]
    nc = s // CHUNK

    def to_chunks(a):
        a = a.reshape((bsz, nh, nc, CHUNK) + a.shape[3:])
        return jnp.moveaxis(a, 2, 0)

    logf = jax.nn.log_sigmoid(f_pre)
    logi = i_pre
    tril = jnp.tril(jnp.ones((CHUNK, CHUNK), dtype=bool))

    def step(carry, inp):
        c_st, n_st, m_st = carry
        qc, kc, vc, li, lf = inp
        b = jnp.cumsum(lf, axis=-1)
        dmat = jnp.where(tril, b[..., :, None] - b[..., None, :] + li[..., None, :], -jnp.inf)
        inter = b + m_st[..., None]
        m_t = jnp.maximum(jnp.max(dmat, axis=-1), inter)
        w_in = jnp.exp(dmat - m_t[..., None])
        g_inter = jnp.exp(inter - m_t)
        p = jnp.einsum('bhtd,bhsd->bhts', qc, kc) * w_in
        num = jnp.einsum('bhts,bhsv->bhtv', p, vc) + g_inter[..., None] * jnp.einsum('bhtd,bhdv->bhtv', qc, c_st)
        den = jnp.sum(p, axis=-1) + g_inter * jnp.einsum('bhtd,bhd->bht', qc, n_st)
        h = num / jnp.maximum(jnp.abs(den), jnp.exp(-m_t))[..., None]
        b_last = b[..., -1]
        ws = b_last[..., None] - b + li
        m_new = jnp.maximum(b_last + m_st, jnp.max(ws, axis=-1))
        decay = jnp.exp(b_last + m_st - m_new)
        ws = jnp.exp(ws - m_new[..., None])
        c_new = decay[..., None, None] * c_st + jnp.einsum('bhs,bhsd,bhsv->bhdv', ws, kc, vc)
        n_new = decay[..., None] * n_st + jnp.einsum('bhs,bhsd->bhd', ws, kc)
        return (c_new, n_new, m_new), h

    init = (jnp.zeros((bsz, nh, dk, dv), jnp.float32),
            jnp.zeros((bsz, nh, dk), jnp.float32),
            jnp.zeros((bsz, nh), jnp.float32))
    _, h = lax.scan(step, init, (to_chunks(q), to_chunks(k), to_chunks(v), to_chunks(logi), to_chunks(logf)))
    h = jnp.moveaxis(h, 0, 2)
    return h.reshape(bsz, nh, s, dv)


def diff_attention(q, k, v, lam, lam_init, subln_g):
    bsz, nh, s = q.shape[:3]
    nb = s // Q_BLOCK
    slopes = alibi_slopes(nh)[:, None, None, None]
    pos = jnp.arange(s)
    scale = B_HEAD_DIM ** -0.5
    qb = jnp.moveaxis(q.reshape(bsz, nh, nb, Q_BLOCK, 2, B_HEAD_DIM), 2, 0)

    def block(args):
        qblk, i = args
        t = i * Q_BLOCK + jnp.arange(Q_BLOCK)
        dist = t[:, None] - pos[None, :]
        logits = jnp.einsum('bhqcd,bhkcd->bhcqk', qblk, k) * scale - slopes * dist.astype(jnp.float32)
        logits = jnp.where(dist >= 0, logits, -jnp.inf)
        p = jax.nn.softmax(logits, axis=-1)
        a = p[:, :, 0] - lam * p[:, :, 1]
        return jnp.einsum('bhqk,bhkv->bhqv', a, v)

    o = lax.map(block, (qb, jnp.arange(nb)))
    o = rmsnorm(o, subln_g) * (1.0 - lam_init)
    o = jnp.transpose(o, (1, 0, 3, 2, 4))
    return o.reshape(bsz, s, nh * 2 * B_HEAD_DIM)


def setup_inputs(seed: int = 0) -> dict:
    key = jax.random.key(seed)
    ks = jax.random.split(key, 20)
    f32 = jnp.float32
    nrm = lambda kk, shape: jax.random.normal(kk, shape, dtype=f32)
    x = nrm(ks[0], (BATCH, SEQ, D_MODEL))
    norm1_g = 1.0 + 0.02 * nrm(ks[1], (DEPTH, D_MODEL))
    w_in = nrm(ks[2], (DEPTH, D_MODEL, P_IN)) * D_MODEL ** -0.5
    i_bias = 0.1 * nrm(ks[3], (DEPTH, A_HEADS))
    f_bias = jnp.linspace(3.0, 6.0, A_HEADS, dtype=f32)[None, :] + 0.1 * nrm(ks[4], (DEPTH, A_HEADS))
    if_bias = jnp.concatenate([i_bias, f_bias], axis=-1)
    qk_conv_w = nrm(ks[5], (DEPTH, A_CONV, 2 * A_QK)) * A_CONV ** -0.5
    qk_conv_b = 0.02 * nrm(ks[6], (DEPTH, 2 * A_QK))
    mlstm_norm_g = 1.0 + 0.02 * nrm(ks[7], (DEPTH, A_V))
    q_norm_g = 1.0 + 0.02 * nrm(ks[8], (DEPTH, B_HEAD_DIM))
    k_norm_g = 1.0 + 0.02 * nrm(ks[9], (DEPTH, B_HEAD_DIM))
    diff_lambda = 0.1 * nrm(ks[10], (DEPTH, 4, B_HEAD_DIM))
    subln_g = 1.0 + 0.02 * nrm(ks[11], (DEPTH, 2 * B_HEAD_DIM))
    w_out = nrm(ks[12], (DEPTH, D_MODEL, D_MODEL)) * D_MODEL ** -0.5
    norm2_g = 1.0 + 0.02 * nrm(ks[13], (DEPTH, D_MODEL))
    w_up = nrm(ks[14], (DEPTH, D_MODEL, 2 * D_FF)) * D_MODEL ** -0.5
    ffn_conv_w = nrm(ks[15], (DEPTH, FFN_CONV, 2 * D_FF)) * FFN_CONV ** -0.5
    ffn_conv_b = 0.02 * nrm(ks[16], (DEPTH, 2 * D_FF))
    w_down = nrm(ks[17], (DEPTH, D_FF, D_MODEL)) * D_FF ** -0.5
    return {'x': x, 'norm1_g': norm1_g, 'w_in': w_in, 'if_bias': if_bias,
            'qk_conv_w': qk_conv_w, 'qk_conv_b': qk_conv_b, 'mlstm_norm_g': mlstm_norm_g,
            'q_norm_g': q_norm_g, 'k_norm_g': k_norm_g, 'diff_lambda': diff_lambda,
            'subln_g': subln_g, 'w_out': w_out, 'norm2_g': norm2_g, 'w_up': w_up,
            'ffn_conv_w': ffn_conv_w, 'ffn_conv_b': ffn_conv_b, 'w_down': w_down}


def reference(x, norm1_g, w_in, if_bias, qk_conv_w, qk_conv_b, mlstm_norm_g, q_norm_g, k_norm_g,
              diff_lambda, subln_g, w_out, norm2_g, w_up, ffn_conv_w, ffn_conv_b, w_down):
    f32 = jnp.float32
    bsz, s, _ = x.shape
    split_idx = np.cumsum(SPLIT_SIZES)[:-1].tolist()
    for l in range(DEPTH):
        h = rmsnorm(x, norm1_g[l])
        proj = h @ w_in[l]
        a_qk, a_v, a_o, a_if, b_q, b_k, b_v, g_a, g_b = jnp.split(proj, split_idx, axis=-1)

        qk = jax.nn.silu(causal_dwconv(a_qk, qk_conv_w[l], qk_conv_b[l]))
        a_q, a_k = jnp.split(qk, 2, axis=-1)
        q_m = a_q.reshape(bsz, s, A_HEADS, A_QK_DIM).transpose(0, 2, 1, 3).astype(f32) * (A_QK_DIM ** -0.5)
        k_m = a_k.reshape(bsz, s, A_HEADS, A_QK_DIM).transpose(0, 2, 1, 3).astype(f32)
        v_m = a_v.reshape(bsz, s, A_HEADS, A_V_DIM).transpose(0, 2, 1, 3).astype(f32)
        gates = (a_if + if_bias[l]).astype(f32).reshape(bsz, s, 2, A_HEADS)
        i_pre = gates[:, :, 0].transpose(0, 2, 1)
        f_pre = gates[:, :, 1].transpose(0, 2, 1)
        h_m = mlstm_chunkwise(q_m, k_m, v_m, i_pre, f_pre)
        h_m = rmsnorm(h_m, mlstm_norm_g[l].reshape(A_HEADS, 1, A_V_DIM))
        h_m = h_m.transpose(0, 2, 1, 3).reshape(bsz, s, A_V)
        y_a = jax.nn.sigmoid(a_o) * h_m.astype(x.dtype)

        q_d = rmsnorm(b_q.reshape(bsz, s, B_HEADS, 2, B_HEAD_DIM).astype(f32), q_norm_g[l]).transpose(0, 2, 1, 3, 4)
        k_d = rmsnorm(b_k.reshape(bsz, s, B_HEADS, 2, B_HEAD_DIM).astype(f32), k_norm_g[l]).transpose(0, 2, 1, 3, 4)
        v_d = b_v.reshape(bsz, s, B_HEADS, 2 * B_HEAD_DIM).transpose(0, 2, 1, 3).astype(f32)
        lam_init = 0.8 - 0.6 * math.exp(-0.3 * l)
        lp = diff_lambda[l].astype(f32)
        lam = jnp.exp(jnp.sum(lp[0] * lp[1])) - jnp.exp(jnp.sum(lp[2] * lp[3])) + lam_init
        y_b = diff_attention(q_d, k_d, v_d, lam, lam_init, subln_g[l]).astype(x.dtype)

        y = jax.nn.sigmoid(g_a) * y_a + jax.nn.sigmoid(g_b) * y_b
        x = x + y @ w_out[l]

        h = rmsnorm(x, norm2_g[l])
        u = causal_dwconv(h @ w_up[l], ffn_conv_w[l], ffn_conv_b[l])
        u_g, u_v = jnp.split(u, 2, axis=-1)
        x = x + (jax.nn.silu(u_g) * u_v) @ w_down[l]
    return x
```

```python
import contextlib
import math
import numpy as np
import concourse.bass as bass
import concourse.mybir as mybir
from concourse.bass_utils import run_bass_kernel_spmd

F32 = mybir.dt.float32
BF16 = mybir.dt.bfloat16
AF = mybir.ActivationFunctionType
ALU = mybir.AluOpType
AX = mybir.AxisListType

D = 2048
DFF = 5632
NFF = DFF // 128
OFF_AV, OFF_AO, OFF_IF, OFF_BQ, OFF_BK, OFF_BV, OFF_GA, OFF_GB = 2048, 4096, 6144, 6152, 8200, 10248, 12296, 14344
PIN = 16392
EPS = 1e-6
LN16 = math.log(16.0)
SAME_ENGINE_SYNC = True
N_DMA_SEMS = 48
PSUM_KEYS = {"tp", "gp", "pB", "pT", "bk", "po", "pl", "pq", "pv", "ptr", "psd", "pnum", "pdc", "pu", "pd"}


class Op:
    __slots__ = ("eng", "fn", "deps", "is_dma", "sem_key", "dma_val", "signals", "sig_val")


class Ctx:
    def __init__(self, nc, es):
        self.nc = nc
        self.engs = ("pe", "act", "dve", "pool", "sp")
        self.eng_sem = {e: es.enter_context(nc.semaphore(f"s_{e}")) for e in self.engs}
        self.eng_cnt = {e: 0 for e in self.engs}
        self.dma_sem = [es.enter_context(nc.semaphore(f"s_dma{i}")) for i in range(N_DMA_SEMS)]
        self.dma_cnt = [0] * N_DMA_SEMS
        self.n_ops = 0
        self.n_waits = 0


class Prog:
    def __init__(self, ctx):
        self.ctx = ctx
        self.ops = []
        self.last_writer = {}
        self.readers = {}
        self.key_slot = {}
        self.slot_cnt = list(ctx.dma_cnt)
        self.dma_last = {}

    def add(self, eng, fn, reads=(), writes=(), dma=False, sem_key=None):
        op = Op()
        op.eng, op.fn, op.is_dma, op.signals, op.sig_val = eng, fn, dma, False, None
        deps = set()
        xr = [k for k in reads if (k if isinstance(k, str) else k[0]) in PSUM_KEYS]
        if xr:
            reads = [k for k in reads if k not in xr]
            writes = list(writes) + [k for k in xr if k not in writes]
        for k in reads:
            w = self.last_writer.get(k)
            if w is not None:
                deps.add(w)
        for k in writes:
            w = self.last_writer.get(k)
            if w is not None:
                deps.add(w)
            rd = self.readers.get(k)
            if rd:
                deps.update(rd[0].values())
                deps.update(rd[1])
        if dma:
            if sem_key is None:
                sem_key = writes[0]
            if sem_key not in self.key_slot:
                assert len(self.key_slot) < N_DMA_SEMS, "too many dma sem keys in phase"
                self.key_slot[sem_key] = len(self.key_slot)
            sl = self.key_slot[sem_key]
            op.sem_key = sl
            prev = self.dma_last.get(sl)
            if prev is not None:
                deps.add(prev)
            self.slot_cnt[sl] += 16
            op.dma_val = self.slot_cnt[sl]
            self.dma_last[sl] = op
        else:
            op.sem_key, op.dma_val = None, None
        deps.discard(op)
        for k in reads:
            rd = self.readers.get(k)
            if rd is None:
                rd = self.readers[k] = ({}, [])
            if dma:
                rd[1].append(op)
            else:
                rd[0][eng] = op
        for k in writes:
            self.last_writer[k] = op
            self.readers[k] = None
        op.deps = deps
        for d in deps:
            if d.is_dma:
                continue
            if d.eng == eng and not dma and (eng == "pe" or not SAME_ENGINE_SYNC):
                continue
            d.signals = True
        self.ops.append(op)
        return op

    def emit(self):
        ctx = self.ctx
        nc = ctx.nc
        per_eng = {e: [op for op in self.ops if op.eng == e] for e in ctx.engs}
        for e in ctx.engs:
            comp = [op for op in per_eng[e] if not op.is_dma]
            if comp:
                comp[-1].signals = True
        cnt = dict(ctx.eng_cnt)
        for op in self.ops:
            if not op.is_dma and op.signals:
                cnt[op.eng] += 1
                op.sig_val = cnt[op.eng]
        final_c = cnt
        final_d = list(self.slot_cnt)
        ctx.n_ops += len(self.ops)

        def run(eng_name, handle):
            seen_c = dict(ctx.eng_cnt)
            seen_d = list(ctx.dma_cnt)
            for op in per_eng[eng_name]:
                need_c, need_d = {}, {}
                for d in op.deps:
                    if d.is_dma:
                        if seen_d[d.sem_key] < d.dma_val:
                            need_d[d.sem_key] = max(need_d.get(d.sem_key, 0), d.dma_val)
                    else:
                        if d.eng == eng_name and not op.is_dma and (eng_name == "pe" or not SAME_ENGINE_SYNC):
                            continue
                        if seen_c[d.eng] < d.sig_val:
                            need_c[d.eng] = max(need_c.get(d.eng, 0), d.sig_val)
                for e, v in need_c.items():
                    handle.wait_ge(ctx.eng_sem[e], v)
                    seen_c[e] = v
                    ctx.n_waits += 1
                for k, v in need_d.items():
                    handle.wait_ge(ctx.dma_sem[k], v)
                    seen_d[k] = v
                    ctx.n_waits += 1
                ins = op.fn(handle)
                if op.is_dma:
                    ins.then_inc(ctx.dma_sem[op.sem_key], 16)
                elif op.signals:
                    ins.then_inc(ctx.eng_sem[op.eng], 1)
            for e in ctx.engs:
                if e != eng_name and seen_c[e] < final_c[e]:
                    handle.wait_ge(ctx.eng_sem[e], final_c[e])
            for k in range(N_DMA_SEMS):
                if seen_d[k] < final_d[k]:
                    handle.wait_ge(ctx.dma_sem[k], final_d[k])

        with nc.Block() as block:
            @block.tensor
            def _(h):
                run("pe", h)

            @block.scalar
            def _(h):
                run("act", h)

            @block.vector
            def _(h):
                run("dve", h)

            @block.gpsimd
            def _(h):
                run("pool", h)

            @block.sync
            def _(h):
                run("sp", h)

        ctx.eng_cnt = final_c
        ctx.dma_cnt = final_d


class Alloc:
    def __init__(self, nc, es, pfx):
        self.nc, self.es, self.pfx = nc, es, pfx

    def sb(self, name, shape, dt):
        return self.es.enter_context(self.nc.sbuf_tensor(f"{self.pfx}_{name}", shape, dt))

    def ps(self, name, shape=(128, 512), dt=F32):
        return self.es.enter_context(self.nc.psum_tensor(f"{self.pfx}_{name}", list(shape), dt))


def wcols(w, c0, n):
    return w[:, c0:c0 + n].rearrange("(c p) n -> p c n", p=128)


def build(NT, NP, debug=False):
    NO = NT - NP
    nc = bass.Bass("TRN2", target_bir_lowering=False)
    din = lambda name, shape: nc.dram_tensor(name, list(shape), F32, kind="ExternalInput").ap()
    xin = din("xin", (NT * 128, D))
    norm1_g = din("norm1_g", (D,))
    w_in = din("w_in", (D, PIN))
    if_bias = din("if_bias", (8,))
    qk_conv_wT = din("qk_conv_wT", (2048, 4))
    qk_conv_b = din("qk_conv_b", (2048,))
    mlstm_norm_g = din("mlstm_norm_g", (2048,))
    q_norm_g = din("q_norm_g", (128,))
    k_norm_g = din("k_norm_g", (128,))
    diff_lambda = din("diff_lambda", (512,))
    subln_g = din("subln_g", (256,))
    w_out = din("w_out", (D, D))
    norm2_g = din("norm2_g", (D,))
    w_up = din("w_up", (D, 2 * DFF))
    ffn_conv_wP = din("ffn_conv_wP", (128, 2 * NFF, 3))
    ffn_conv_bP = din("ffn_conv_bP", (128, 2 * NFF))
    w_down = din("w_down", (DFF, D))
    c_ident = din("c_ident", (128, 128))
    c_maskU = din("c_maskU", (128, 128))
    c_keep = din("c_keep", (128, 1))
    c_alibi = din("c_alibi", (128, 8 * NT))
    c_alibiW = din("c_alibiW", (128, 8 * (NT + 4)))
    out = nc.dram_tensor("out", [NO * 128, D], F32, kind="ExternalOutput").ap()

    hT_d = nc.dram_tensor("hT_d", [NT, 128, 16, 128], BF16).ap()
    yb_d = nc.dram_tensor("yb_d", [NO * 128, D], F32).ap()
    yT_d = nc.dram_tensor("yT_d", [NO, 128, 16, 128], BF16).ap()
    xm_d = nc.dram_tensor("xm_d", [NO * 128, D], F32).ap()
    h2T_d = nc.dram_tensor("h2T_d", [NO, 128, 16, 128], BF16).ap()
    wup_d = nc.dram_tensor("wup_d", [NFF, 128, 16, 256], BF16).ap()
    wdn_d = nc.dram_tensor("wdn_d", [NFF, 128, D], BF16).ap()
    dbg = {}
    if debug:
        dbg["y"] = nc.dram_tensor("dbg_y", [NO * 128, D], F32, kind="ExternalOutput").ap()
        dbg["yb"] = nc.dram_tensor("dbg_yb", [NO * 128, D], F32, kind="ExternalOutput").ap()
        dbg["xm"] = nc.dram_tensor("dbg_xm", [NO * 128, D], F32, kind="ExternalOutput").ap()

    ges = contextlib.ExitStack()
    with ges:
        ctx = Ctx(nc, ges)
        G = Alloc(nc, ges, "g")
        ident_f = G.sb("ident_f", [128, 128], F32)
        ident_b = G.sb("ident_b", [128, 128], BF16)
        maskU = G.sb("maskU", [128, 128], F32)
        ones_f = G.sb("ones_f", [128, 128], F32)
        ones_b = G.sb("ones_b", [128, 128], BF16)
        keepc = G.sb("keepc", [128, 1], F32)
        onec = G.sb("onec", [128, 1], F32)
        onecb = G.sb("onecb", [128, 1], BF16)
        keepcb = G.sb("keepcb", [128, 1], BF16)
        epsc = G.sb("epsc", [128, 1], F32)
        ln16c = G.sb("ln16c", [128, 1], F32)
        alibi = G.sb("alibi", [128, 8 * NT], F32)
        alibiW = G.sb("alibiW", [128, 8 * (NT + 4)], F32)
        gq = G.sb("gq", [128, 1], F32)
        gk = G.sb("gk", [128, 1], F32)
        neglam = G.sb("neglam", [128, 1], F32)
        sublg = G.sb("sublg", [128, 256], F32)
        ifb = G.sb("ifb", [128, 8], F32)
        Graw = G.sb("Graw", [128, NT, 8], F32)
        WT = G.sb("WT", [128, 4, NT], F32)
        CL = G.sb("CL", [128, 4, NT], F32)
        DEL = G.sb("DEL", [128, 4, NT], F32)

        with contextlib.ExitStack() as es:
            A = Alloc(nc, es, "c")
            P = Prog(ctx)
            gbc = A.sb("gbc", [128, 2, 128], F32)
            gmx = A.sb("gmx", [128, 2], F32)
            negM = A.sb("negM", [128, 1], F32)
            lbc = A.sb("lbc", [128, 4, 128], F32)
            lpr = A.sb("lpr", [128, 2, 128], F32)
            lsum = A.sb("lsum", [128, 2], F32)
            P.add("sp", lambda h: h.dma_start(out=ident_f[:], in_=c_ident[:, :]), writes=["ident_f"], dma=True)
            P.add("sp", lambda h: h.dma_start(out=maskU[:], in_=c_maskU[:, :]), writes=["maskU"], dma=True)
            P.add("sp", lambda h: h.dma_start(out=keepc[:], in_=c_keep[:, :]), writes=["keepc"], dma=True)
            P.add("sp", lambda h: h.dma_start(out=alibi[:], in_=c_alibi[:, :]), writes=["alibi"], dma=True)
            P.add("sp", lambda h: h.dma_start(out=alibiW[:], in_=c_alibiW[:, :]), writes=["alibiW"], dma=True)
            P.add("sp", lambda h: h.dma_start(out=gq[:], in_=q_norm_g.rearrange("(p o) -> p o", o=1)), writes=["gq"], dma=True)
            P.add("sp", lambda h: h.dma_start(out=gk[:], in_=k_norm_g.rearrange("(p o) -> p o", o=1)), writes=["gk"], dma=True)
            P.add("sp", lambda h: h.dma_start(out=gbc[:, 0, :], in_=q_norm_g.partition_broadcast(128)), writes=["gbc0"], dma=True)
            P.add("sp", lambda h: h.dma_start(out=gbc[:, 1, :], in_=k_norm_g.partition_broadcast(128)), writes=["gbc1"], dma=True)
            P.add("sp", lambda h: h.dma_start(out=lbc[:].rearrange("p a b -> p (a b)"), in_=diff_lambda.partition_broadcast(128)), writes=["lbc"], dma=True)
            P.add("sp", lambda h: h.dma_start(out=sublg[:], in_=subln_g.partition_broadcast(128)), writes=["sublg"], dma=True)
            P.add("sp", lambda h: h.dma_start(out=ifb[:], in_=if_bias.partition_broadcast(128)), writes=["ifb"], dma=True)
            P.add("dve", lambda h: h.tensor_copy(out=ident_b[:], in_=ident_f[:]), reads=["ident_f"], writes=["ident_b"])
            P.add("dve", lambda h: h.memset(ones_f[:], 1.0), writes=["ones_f"])
            P.add("dve", lambda h: h.memset(ones_b[:], 1.0), writes=["ones_b"])
            P.add("dve", lambda h: h.memset(onec[:], 1.0), writes=["onec"])
            P.add("dve", lambda h: h.memset(onecb[:], 1.0), writes=["onecb"])
            P.add("dve", lambda h: h.memset(epsc[:], EPS), writes=["epsc"])
            P.add("dve", lambda h: h.memset(ln16c[:], LN16), writes=["ln16c"])
            P.add("dve", lambda h: h.tensor_copy(out=keepcb[:], in_=keepc[:]), reads=["keepc"], writes=["keepcb"])
            P.add("dve", lambda h: h.scalar_tensor_tensor(out=gbc[:], in0=gbc[:], scalar=-1.0, in1=gbc[:], op0=ALU.mult, op1=ALU.max), reads=["gbc0", "gbc1"], writes=["gbc0", "gbc1"])
            P.add("dve", lambda h: h.tensor_reduce(out=gmx[:], in_=gbc[:], axis=AX.X, op=ALU.max), reads=["gbc0", "gbc1"], writes=["gmx"])
            P.add("dve", lambda h: h.scalar_tensor_tensor(out=negM[:], in0=gmx[:, 0:1], scalar=-math.sqrt(128.0), in1=gmx[:, 1:2],
                                                         op0=ALU.mult, op1=ALU.mult), reads=["gmx"], writes=["negM"])
            P.add("dve", lambda h: h.tensor_scalar(out=alibi[:], in0=alibi[:], scalar1=negM[:], scalar2=None, op0=ALU.add),
                  reads=["alibi", "negM"], writes=["alibi"])
            P.add("dve", lambda h: h.tensor_scalar(out=alibiW[:], in0=alibiW[:], scalar1=negM[:], scalar2=None, op0=ALU.add),
                  reads=["alibiW", "negM"], writes=["alibiW"])
            P.add("dve", lambda h: h.tensor_tensor(out=lpr[:, 0, :], in0=lbc[:, 0, :], in1=lbc[:, 1, :], op=ALU.mult), reads=["lbc"], writes=["lpr0"])
            P.add("dve", lambda h: h.tensor_tensor(out=lpr[:, 1, :], in0=lbc[:, 2, :], in1=lbc[:, 3, :], op=ALU.mult), reads=["lbc"], writes=["lpr1"])
            P.add("dve", lambda h: h.tensor_reduce(out=lsum[:], in_=lpr[:], axis=AX.X, op=ALU.add), reads=["lpr0", "lpr1"], writes=["lsum"])
            P.add("act", lambda h: h.activation(out=lsum[:], in_=lsum[:], func=AF.Exp), reads=["lsum"], writes=["lsum"])
            P.add("dve", lambda h: h.scalar_tensor_tensor(out=neglam[:], in0=lsum[:, 1:2], scalar=-0.2, in1=lsum[:, 0:1],
                                                         op0=ALU.add, op1=ALU.subtract), reads=["lsum"], writes=["neglam"])
            P.add("dve", lambda h: h.tensor_scalar(out=sublg[:], in0=sublg[:], scalar1=0.8, scalar2=None, op0=ALU.mult), reads=["sublg"], writes=["sublg"])
            P.emit()

        with contextlib.ExitStack() as es:
            A = Alloc(nc, es, "p0")
            P = Prog(ctx)
            xt = [A.sb(f"xt{i}", [128, D], F32) for i in range(2)]
            g1bc = A.sb("g1bc", [128, D], F32)
            hn = [A.sb(f"hn{i}", [128, D], BF16) for i in range(2)]
            ss = [A.sb(f"ss{i}", [128, 1], F32) for i in range(2)]
            rstd = [A.sb(f"rstd{i}", [128, 1], F32) for i in range(2)]
            junk = A.sb("junk", [128, D], BF16)
            hT = [A.sb(f"hT{i}", [128, 16, 128], BF16) for i in range(2)]
            wif = A.sb("wif", [128, 16, 8], BF16)
            tp = [A.ps(f"tp{i}", [128, 8, 128], BF16) for i in range(2)]
            gp = A.ps("gp")
            P.add("sp", lambda h: h.dma_start(out=g1bc[:], in_=norm1_g.partition_broadcast(128)), writes=["g1bc"], dma=True)
            wif_f = A.sb("wif_f", [128, 16, 8], F32)
            P.add("sp", lambda h: h.dma_start(out=wif_f[:], in_=wcols(w_in, OFF_IF, 8)), writes=["wif_f"], dma=True)
            P.add("pool", lambda h: h.tensor_copy(out=wif[:], in_=wif_f[:]), reads=["wif_f"], writes=["wif"])
            def p0_load(t):
                P.add("sp", lambda h, t=t: h.dma_start(out=xt[t % 2][:], in_=xin[t * 128:(t + 1) * 128, :]), writes=[("xt", t % 2)], dma=True)

            p0_load(0)
            for t in range(NT):
                s = t % 2
                if t + 1 < NT:
                    p0_load(t + 1)
                P.add("act", lambda h, s=s: h.activation(out=junk[:], in_=xt[s][:], func=AF.Square, accum_out=ss[s][:]),
                      reads=[("xt", s)], writes=["junk", ("ss", s)])
                P.add("act", lambda h, s=s: h.activation(out=rstd[s][:], in_=ss[s][:], func=AF.Sqrt, scale=1.0 / D, bias=epsc[:]),
                      reads=[("ss", s)], writes=[("rstd", s)])
                P.add("dve", lambda h, s=s: h.reciprocal(out=rstd[s][:], in_=rstd[s][:]), reads=[("rstd", s)], writes=[("rstd", s)])
                P.add("dve", lambda h, s=s: h.scalar_tensor_tensor(out=hn[s][:], in0=xt[s][:], scalar=rstd[s][:], in1=g1bc[:], op0=ALU.mult, op1=ALU.mult),
                      reads=[("xt", s), ("rstd", s), "g1bc"], writes=[("hn", s)])
                for half in range(2):
                    for c in range(8):
                        cc = half * 8 + c
                        P.add("pe", lambda h, s=s, half=half, c=c, cc=cc: h.transpose(out=tp[half][:, c, :], in_=hn[s][:, cc * 128:(cc + 1) * 128], identity=ident_b[:]),
                              reads=[("hn", s)], writes=[("tp", half)])
                    if half == 0:
                        P.add("act", lambda h, s=s: h.activation(out=hT[s][:, 0:8, :], in_=tp[0][:], func=AF.Copy), reads=[("tp", 0)], writes=[("hT", s, 0)])
                    else:
                        P.add("dve", lambda h, s=s: h.tensor_copy(out=hT[s][:, 8:16, :], in_=tp[1][:]), reads=[("tp", 1)], writes=[("hT", s, 1)])
                P.add("sp", lambda h, t=t, s=s: h.dma_start(out=hT_d[t], in_=hT[s][:]), reads=[("hT", s, 0), ("hT", s, 1)], writes=[("hT_d", t)],
                      dma=True, sem_key=("hTst", s))
                for kc in range(16):
                    P.add("pe", lambda h, s=s, kc=kc: h.matmul(gp[:, 0:8], lhsT=hT[s][:, kc, :], rhs=wif[:, kc, :], start=(kc == 0), stop=(kc == 15)),
                          reads=[("hT", s, 0), ("hT", s, 1), "wif"], writes=["gp"])
                P.add("dve", lambda h, t=t: h.tensor_copy(out=Graw[:, t, :], in_=gp[:, 0:8]), reads=["gp"], writes=[("Graw", t)])
            P.emit()

        with contextlib.ExitStack() as es:
            A = Alloc(nc, es, "pg")
            P = Prog(ctx)
            N4 = 4 * NT
            LI = A.sb("LI", [128, 4, NT], F32)
            LF = A.sb("LF", [128, 4, NT], F32)
            Bc = A.sb("Bc", [128, 4, NT], F32)
            BL = A.sb("BL", [128, 4, NT], F32)
            Aa = A.sb("Aa", [128, 4, NT], F32)
            AMX = A.sb("AMX", [128, 4, NT], F32)
            MU = A.sb("MU", [128, 4, NT], F32)
            MST = A.sb("MST", [128, 4, NT + 1], F32)
            TMP = A.sb("TMP", [128, 4, NT], F32)
            negfb = A.sb("negfb", [128, 4], F32)
            amax = A.sb("amax", [128, 1], F32)
            diag = A.sb("diag", [128, 128], F32)
            pB = A.ps("pB")
            pT = A.ps("pT")
            flat = lambda tns: tns[:].rearrange("p a b -> p (a b)")
            P.add("dve", lambda h: h.tensor_scalar(out=negfb[:], in0=ifb[:, 4:8], scalar1=-1.0, scalar2=None, op0=ALU.mult), writes=["negfb"])
            for hd in range(4):
                P.add("dve", lambda h, hd=hd: h.tensor_scalar(out=LI[:, hd, :], in0=Graw[:, :, hd], scalar1=ifb[:, hd:hd + 1], scalar2=None, op0=ALU.add),
                      writes=[("LI", hd)])
                P.add("act", lambda h, hd=hd: h.activation(out=LF[:, hd, :], in_=Graw[:, :, 4 + hd], func=AF.Exp, scale=-1.0, bias=negfb[:, hd:hd + 1]),
                      reads=["negfb"], writes=[("LF", hd)])
            allk = lambda nm: [(nm, hd) for hd in range(4)]
            P.add("act", lambda h: h.activation(out=flat(LF), in_=flat(LF), func=AF.Ln, bias=onec[:]), reads=allk("LF"), writes=allk("LF"))
            P.add("dve", lambda h: h.tensor_scalar(out=flat(LF), in0=flat(LF), scalar1=-1.0, scalar2=None, op0=ALU.mult), reads=allk("LF"), writes=allk("LF"))
            nchunk = (N4 + 511) // 512
            for ci in range(nchunk):
                c0, c1 = ci * 512, min(N4, ci * 512 + 512)
                P.add("pe", lambda h, c0=c0, c1=c1: h.matmul(pB[:, 0:c1 - c0], lhsT=maskU[:], rhs=flat(LF)[:, c0:c1], start=True, stop=True), reads=allk("LF"), writes=["pB"])
                P.add("dve", lambda h, c0=c0, c1=c1: h.tensor_copy(out=flat(Bc)[:, c0:c1], in_=pB[:, 0:c1 - c0]), reads=["pB"], writes=[("Bc", ci)])
                P.add("pe", lambda h, c0=c0, c1=c1: h.matmul(pB[:, 0:c1 - c0], lhsT=ones_f[:], rhs=flat(LF)[:, c0:c1], start=True, stop=True), reads=allk("LF"), writes=["pB"])
                P.add("dve", lambda h, c0=c0, c1=c1: h.tensor_copy(out=flat(BL)[:, c0:c1], in_=pB[:, 0:c1 - c0]), reads=["pB"], writes=[("BL", ci)])
            bk = [("Bc", ci) for ci in range(nchunk)]
            blk = [("BL", ci) for ci in range(nchunk)]
            P.add("dve", lambda h: h.tensor_tensor(out=flat(Aa), in0=flat(LI), in1=flat(Bc), op=ALU.subtract), reads=allk("LI") + bk, writes=["Aa"])
            for c0 in range(0, N4, 128):
                w = min(128, N4 - c0)
                P.add("pe", lambda h, c0=c0, w=w: h.transpose(out=pT[0:w, 0:128], in_=flat(Aa)[:, c0:c0 + w], identity=ident_f[:]), reads=["Aa"], writes=["pT"])
                P.add("dve", lambda h, w=w: h.tensor_reduce(out=amax[0:w, :], in_=pT[0:w, 0:128], axis=AX.X, op=ALU.max), reads=["pT"], writes=["amax"])
                P.add("dve", lambda h, w=w: h.tensor_scalar(out=diag[0:w, 0:w], in0=ident_f[0:w, 0:w], scalar1=amax[0:w, :], scalar2=None, op0=ALU.mult),
                      reads=["amax"], writes=["diag"])
                P.add("pe", lambda h, w=w: h.matmul(pB[:, 0:w], lhsT=ones_f[0:w, :], rhs=diag[0:w, 0:w], start=True, stop=True), reads=["diag"], writes=["pB"])
                P.add("dve", lambda h, c0=c0, w=w: h.tensor_copy(out=flat(AMX)[:, c0:c0 + w], in_=pB[:, 0:w]), reads=["pB"], writes=[("AMX", c0)])
            amk = [("AMX", c0) for c0 in range(0, N4, 128)]
            P.add("dve", lambda h: h.memset(MST[:, :, 0:1], 0.0), writes=["MST"])
            for t in range(NT):
                P.add("dve", lambda h, t=t: h.tensor_tensor(out=MU[:, :, t:t + 1], in0=MST[:, :, t:t + 1], in1=AMX[:, :, t:t + 1], op=ALU.max),
                      reads=["MST"] + amk, writes=["MU"])
                P.add("dve", lambda h, t=t: h.tensor_tensor(out=MST[:, :, t + 1:t + 2], in0=MU[:, :, t:t + 1], in1=BL[:, :, t:t + 1], op=ALU.add),
                      reads=["MU"] + blk, writes=["MST"])
            P.add("dve", lambda h: h.tensor_tensor(out=TMP[:], in0=MST[:, :, 0:NT], in1=MU[:], op=ALU.subtract), reads=["MST", "MU"], writes=["TMP"])
            P.add("act", lambda h: h.activation(out=flat(DEL), in_=flat(TMP), func=AF.Exp), reads=["TMP"], writes=["DEL"])
            P.add("dve", lambda h: h.tensor_tensor(out=flat(TMP), in0=flat(Aa), in1=flat(MU), op=ALU.subtract), reads=["Aa", "MU", "TMP"], writes=["TMP"])
            P.add("act", lambda h: h.activation(out=flat(WT), in_=flat(TMP), func=AF.Exp), reads=["TMP"], writes=["WT"])
            P.add("dve", lambda h: h.tensor_tensor(out=flat(TMP), in0=flat(Bc), in1=flat(MU), op=ALU.add), reads=bk + ["MU", "TMP"], writes=["TMP"])
            P.add("act", lambda h: h.activation(out=flat(CL), in_=flat(TMP), func=AF.Exp, scale=-1.0, bias=ln16c[:]), reads=["TMP"], writes=["CL"])
            P.emit()

        blocks = [(b0, min(b0 + 4, NT)) for b0 in range(0, NT, 4)]
        TSKIP = 100.0
        dstack = contextlib.ExitStack()
        DA = Alloc(nc, dstack, "dw")
        WallD = [DA.sb(f"Wall{i}", [128, 16, 1024], BF16) for i in range(2)]
        wstgD = [DA.sb(f"wstg{i}", [128, 16, 64], F32) for i in range(2)]

        def wchunks(hd_):
            bases = (OFF_BQ + hd_ * 256, OFF_BK + hd_ * 256, OFF_BV + hd_ * 256, OFF_GB + hd_ * 256)
            return [(i * 64, bases[i // 4] + (i % 4) * 64) for i in range(16)]

        for hd in range(8):
            slope = 2.0 ** -(hd + 1)
            wide = slope <= 0.125
            kb = int(math.ceil(TSKIP / (slope * 128.0)))
            q_first = (NP // 4) * 4 if wide else NP
            first_needed = max(0, q_first - kb)
            blocks_h = [blk for blk in blocks if blk[1] > first_needed]
            with contextlib.ExitStack() as es:
                A = Alloc(nc, es, f"d{hd}")
                P = Prog(ctx)
                Wall = WallD[hd % 2]
                hTb = [A.sb(f"hTb{i}", [128, 4, 16, 128], BF16) for i in range(2)]
                kT = A.sb("kT", [128, 2, NT * 128], BF16)
                va = A.sb("va", [128, NT, 258], BF16)
                qT = A.sb("qT", [128, 2, 512], BF16)
                sq = A.sb("sq", [128, 512], BF16)
                rs = A.sb("rs", [128, 512], F32)
                gb = [A.sb(f"gb{i}", [128, 256], F32) for i in range(4)]
                pt = [A.sb(f"pt{i}", [128, 512], BF16) for i in range(3)]
                o_sb = A.sb("o_sb", [128, 256], F32)
                o_junk = A.sb("o_junk", [128, 256], F32)
                yb_sb = [A.sb(f"yb{i}", [128, 256], F32) for i in range(2)]
                r12 = A.sb("r12", [128, 8], F32)
                oss = A.sb("oss", [128, 1], F32)
                bank = [A.ps(f"bk{i}", [128, 4, 128]) for i in range(3)]
                po = [A.ps(f"po{i}") for i in range(4)]
                pl = A.ps("pl")
                Wq, Wk, Wv, Wg = (Wall[:, :, i * 256:(i + 1) * 256] for i in range(4))
                wctr = [0]

                def wstep(hd_, dc, sc, cur):
                    i = wctr[0] % 2
                    wctr[0] += 1
                    dstW = WallD[hd_ % 2]
                    P.add("sp", lambda h: h.dma_start(out=wstgD[i][:], in_=wcols(w_in, sc, 64)), writes=[("wstgD", i)], dma=True)
                    P.add("pool", lambda h: h.tensor_copy(out=dstW[:, :, dc:dc + 64], in_=wstgD[i][:]), reads=[("wstgD", i)],
                          writes=[("W", dc // 256)] if cur else [("Wn", dc)])

                if hd == 0:
                    for dc, sc in wchunks(0):
                        wstep(0, dc, sc, True)
                pf = wchunks(hd + 1) if hd + 1 < 8 else []
                pf_per_blk = -(-len(pf) // len(blocks_h)) if pf else 0
                P.add("dve", lambda h: h.memset(va[:, :, 256:257], 1.0), writes=[("vaone", 0)])
                if NP > 0:
                    P.add("dve", lambda h: h.tensor_scalar(out=va[:, 0:NP, 256:257], in0=va[:, 0:NP, 256:257], scalar1=keepc[:], scalar2=None, op0=ALU.mult),
                          reads=[("vaone", 0)], writes=[("vaone", 0)])

                def load_blk(bi):
                    b0, b1 = blocks_h[bi]
                    s = bi % 2
                    P.add("sp", lambda h, b0=b0, b1=b1, s=s: h.dma_start(out=hTb[s][:, 0:b1 - b0], in_=hT_d[b0:b1].rearrange("t p c n -> p t c n")),
                          writes=[("hTb", s)], dma=True)

                rawsb = [A.sb(f"rawsb{i}", [128, 512], F32) for i in range(2)]
                rot = [0]
                prot = [0]

                def nextbank():
                    rot[0] ^= 2
                    return rot[0]

                def qknorm(bkr, nt, gcol, dst, dkeys, ri):
                    n = nt * 128
                    srcf = bank[bkr][:, 0:nt, :]
                    v3 = lambda ap: ap.rearrange("p (a b) -> p a b", b=128)
                    P.add("act", lambda h: h.activation(out=v3(sq[:, 0:n]), in_=srcf, func=AF.Square), reads=[("bk", bkr)], writes=["sq"])
                    P.add("dve", lambda h: h.tensor_copy(out=v3(rawsb[ri][:, 0:n]), in_=srcf), reads=[("bk", bkr)], writes=[("rawsb", ri)])
                    P.add("pe", lambda h: h.matmul(bank[1][:, 0:nt, :], lhsT=ones_b[:], rhs=v3(sq[:, 0:n]), start=True, stop=True),
                          reads=["sq"], writes=[("bk", 1)])
                    P.add("act", lambda h: h.activation(out=v3(rs[:, 0:n]), in_=bank[1][:, 0:nt, :], func=AF.Sqrt, scale=1.0 / 128, bias=epsc[:]),
                          reads=[("bk", 1)], writes=["rs"])
                    P.add("dve", lambda h: h.reciprocal(out=rs[:, 0:n], in_=rs[:, 0:n]), reads=["rs"], writes=["rs"])
                    P.add("dve", lambda h: h.scalar_tensor_tensor(out=dst, in0=rawsb[ri][:, 0:n], scalar=gcol, in1=rs[:, 0:n], op0=ALU.mult, op1=ALU.mult),
                          reads=[("rawsb", ri), "rs"], writes=dkeys)

                def attention(q0, q1, jmin, qoff):
                    nq = q1 - q0
                    units = []
                    for kt in range(max(0, q0 - kb), q1):
                        jlo = max(jmin, kt - q0, 0)
                        if jlo >= nq:
                            continue
                        for c in range(2):
                            units.append((kt, c, jlo))
                    kt_first = units[0][0]

                    def e_st(ui):
                        kt, c, jlo = units[ui]
                        s3 = ui % 3
                        P.add("pe", lambda h: h.matmul(bank[s3][:, jlo:nq, :], lhsT=kT[:, c, kt * 128:(kt + 1) * 128],
                                                       rhs=qT[:, c, qoff + jlo * 128:qoff + nq * 128].rearrange("p (a b) -> p a b", b=128), start=True, stop=True),
                              reads=[("kT", c, kt), ("qT", c)], writes=[("bk", s3)])

                    def e_act(ui):
                        kt, c, jlo = units[ui]
                        s3 = ui % 3
                        if wide:
                            col = hd * (NT + 4) + (q0 + 3 - kt)
                            bcol = alibiW[:, col:col + 1]
                        else:
                            col = hd * NT + (q0 - kt)
                            bcol = alibi[:, col:col + 1]
                        P.add("act", lambda h: h.activation(out=pt[s3][:, jlo * 128:nq * 128].rearrange("p (a b) -> p a b", b=128), in_=bank[s3][:, jlo:nq, :], func=AF.Exp,
                                                            scale=128.0 ** -0.5, bias=bcol), reads=[("bk", s3)], writes=[("pt", s3)])
                        jd = kt - q0
                        if jd >= jlo:
                            P.add("dve", lambda h: h.tensor_tensor(out=pt[s3][:, jd * 128:(jd + 1) * 128], in0=pt[s3][:, jd * 128:(jd + 1) * 128], in1=maskU[:], op=ALU.mult),
                                  reads=[("pt", s3)], writes=[("pt", s3)])

                    pv_list = [(ui, j) for ui in range(len(units)) for j in range(units[ui][2], nq)]
                    first_b, last_b = {}, {}
                    for idx, (ui, j) in enumerate(pv_list):
                        bk_ = (units[ui][1] * 4 + j) // 2
                        first_b.setdefault(bk_, idx)
                        last_b[bk_] = idx
                    pv_idx = {k_: i_ for i_, k_ in enumerate(pv_list)}

                    def e_pv(ui):
                        kt, c, jlo = units[ui]
                        s3 = ui % 3
                        for j in range(jlo, nq):
                            a = c * 4 + j
                            idx = pv_idx[(ui, j)]
                            acc = po[a // 2][:, (a % 2) * 256:(a % 2) * 256 + 256]
                            st_, sp_ = (first_b[a // 2] == idx), (last_b[a // 2] == idx)
                            P.add("pe", lambda h, j=j, acc=acc, st_=st_, sp_=sp_: h.matmul(acc, lhsT=pt[s3][:, j * 128:(j + 1) * 128], rhs=va[:, kt, 0:256], start=st_, stop=sp_),
                                  reads=[("pt", s3), ("va", kt)], writes=[("po", a // 2)])
                            st_, sp_ = (idx == 0), (idx == len(pv_list) - 1)
                            P.add("pe", lambda h, j=j, a=a, st_=st_, sp_=sp_: h.matmul(pl[:, a:a + 1], lhsT=pt[s3][:, j * 128:(j + 1) * 128], rhs=va[:, kt, 256:257], start=st_, stop=sp_),
                                  reads=[("pt", s3), ("vaone", 0)], writes=["pl"])

                    LA = 2
                    for ui in range(min(LA, len(units))):
                        e_st(ui)
                    for ui in range(len(units)):
                        e_act(ui)
                        if ui + LA < len(units):
                            e_st(ui + LA)
                        e_pv(ui)
                    for j in range(jmin, nq):
                        t = q0 + j
                        to = t - NP
                        ys = to % 2
                        a1, a2 = j, 4 + j
                        P.add("dve", lambda h, j=j, a1=a1: h.reciprocal(out=r12[:, j:j + 1], in_=pl[:, a1:a1 + 1]), reads=["pl"], writes=[("r1", j)])
                        P.add("dve", lambda h, j=j, a2=a2: h.reciprocal(out=r12[:, 4 + j:5 + j], in_=pl[:, a2:a2 + 1]), reads=["pl"], writes=[("r2", j)])
                        P.add("dve", lambda h, j=j: h.tensor_tensor(out=r12[:, 4 + j:5 + j], in0=r12[:, 4 + j:5 + j], in1=neglam[:], op=ALU.mult), reads=[("r2", j)], writes=[("r2", j)])
                        acc1 = po[a1 // 2][:, (a1 % 2) * 256:(a1 % 2) * 256 + 256]
                        acc2 = po[a2 // 2][:, (a2 % 2) * 256:(a2 % 2) * 256 + 256]
                        P.add("act", lambda h, j=j, acc1=acc1: h.activation(out=o_sb[:], in_=acc1, func=AF.Copy, scale=r12[:, j:j + 1]), reads=[("po", a1 // 2), ("r1", j)], writes=["o_sb"])
                        P.add("dve", lambda h, j=j, acc2=acc2: h.scalar_tensor_tensor(out=o_sb[:], in0=acc2, scalar=r12[:, 4 + j:5 + j], in1=o_sb[:], op0=ALU.mult, op1=ALU.add),
                              reads=[("po", a2 // 2), ("r2", j), "o_sb"], writes=["o_sb"])
                        P.add("act", lambda h: h.activation(out=o_junk[:], in_=o_sb[:], func=AF.Square, accum_out=oss[:]), reads=["o_sb"], writes=["o_junk", "oss"])
                        P.add("act", lambda h: h.activation(out=oss[:], in_=oss[:], func=AF.Sqrt, scale=1.0 / 256, bias=epsc[:]), reads=["oss"], writes=["oss"])
                        P.add("dve", lambda h: h.reciprocal(out=oss[:], in_=oss[:]), reads=["oss"], writes=["oss"])
                        P.add("dve", lambda h, ys=ys: h.scalar_tensor_tensor(out=yb_sb[ys][:], in0=o_sb[:], scalar=oss[:], in1=sublg[:], op0=ALU.mult, op1=ALU.mult),
                              reads=["o_sb", "oss"], writes=[("yb", ys)])
                        tl = t % 4
                        P.add("dve", lambda h, ys=ys, tl=tl: h.tensor_tensor(out=yb_sb[ys][:], in0=yb_sb[ys][:], in1=gb[tl][:], op=ALU.mult),
                              reads=[("yb", ys), ("gb", tl)], writes=[("yb", ys)])
                        P.add("sp", lambda h, ys=ys, to=to: h.dma_start(out=yb_d[to * 128:(to + 1) * 128, hd * 256:(hd + 1) * 256], in_=yb_sb[ys][:]),
                              reads=[("yb", ys)], writes=[("yb_d", to)], dma=True, sem_key=("ybst", ys))

                def do_block(bi, b0, b1):
                    nt = b1 - b0
                    s = bi % 2
                    if bi + 1 < len(blocks_h):
                        load_blk(bi + 1)
                    for _ in range(pf_per_blk):
                        if pf:
                            dc_, sc_ = pf.pop(0)
                            wstep(hd + 1, dc_, sc_, False)
                    hk = [("hTb", s)]
                    for c in range(2):
                        bkr = nextbank()
                        for kc in range(16):
                            P.add("pe", lambda h, c=c, kc=kc, bkr=bkr: h.matmul(bank[bkr][:, 0:nt, :], lhsT=Wk[:, kc, c * 128:(c + 1) * 128], rhs=hTb[s][:, 0:nt, kc, :],
                                                                              start=(kc == 0), stop=(kc == 15)), reads=hk + [("W", 1)], writes=[("bk", bkr)])
                        qknorm(bkr, nt, gk[:], kT[:, c, b0 * 128:b1 * 128], [("kT", c, t) for t in range(b0, b1)], c)
                    for t in range(b0, b1):
                        tl = t - b0
                        pi = prot[0] % 4
                        prot[0] += 1
                        for kc in range(16):
                            P.add("pe", lambda h, tl=tl, kc=kc, pi=pi: h.matmul(po[pi][:, 0:256], lhsT=hTb[s][:, tl, kc, :], rhs=Wv[:, kc, :],
                                                                              start=(kc == 0), stop=(kc == 15)), reads=hk + [("W", 2)], writes=[("po", pi)])
                        if t < NP:
                            P.add("act", lambda h, t=t, pi=pi: h.activation(out=va[:, t, 0:256], in_=po[pi][:, 0:256], func=AF.Copy, scale=keepc[:]),
                                  reads=[("po", pi)], writes=[("va", t)])
                        else:
                            P.add("dve", lambda h, t=t, pi=pi: h.tensor_copy(out=va[:, t, 0:256], in_=po[pi][:, 0:256]),
                                  reads=[("po", pi)], writes=[("va", t)])
                    if b1 <= NP:
                        return
                    for c in range(2):
                        bkr = nextbank()
                        for kc in range(16):
                            P.add("pe", lambda h, c=c, kc=kc, bkr=bkr: h.matmul(bank[bkr][:, 0:nt, :], lhsT=Wq[:, kc, c * 128:(c + 1) * 128], rhs=hTb[s][:, 0:nt, kc, :],
                                                                              start=(kc == 0), stop=(kc == 15)), reads=hk + [("W", 0)], writes=[("bk", bkr)])
                        qknorm(bkr, nt, gq[:], qT[:, c, 0:nt * 128], [("qT", c)], c)
                    for t in range(max(b0, NP), b1):
                        tl = t - b0
                        pi = prot[0] % 4
                        prot[0] += 1
                        for kc in range(16):
                            P.add("pe", lambda h, tl=tl, kc=kc, pi=pi: h.matmul(po[pi][:, 0:256], lhsT=hTb[s][:, tl, kc, :], rhs=Wg[:, kc, :],
                                                                              start=(kc == 0), stop=(kc == 15)), reads=hk + [("W", 3)], writes=[("po", pi)])
                        P.add("act", lambda h, tl=tl, pi=pi: h.activation(out=gb[tl][:], in_=po[pi][:, 0:256], func=AF.Sigmoid),
                              reads=[("po", pi)], writes=[("gb", tl)])
                    jmin = max(0, NP - b0)
                    if wide:
                        attention(b0, b1, jmin, 0)
                    else:
                        for t in range(max(b0, NP), b1):
                            attention(t, t + 1, 0, (t - b0) * 128)

                load_blk(0)
                for bi, (b0, b1) in enumerate(blocks_h):
                    do_block(bi, b0, b1)
                assert not pf
                P.emit()

        dstack.close()
        for hm in range(4):
            with contextlib.ExitStack() as es:
                A = Alloc(nc, es, f"m{hm}")
                P = Prog(ctx)
                Wq = A.sb("Wq", [128, 16, 256], BF16)
                Wk = A.sb("Wk", [128, 16, 256], BF16)
                Wv = A.sb("Wv", [128, 16, 512], BF16)
                Wo = A.sb("Wo", [128, 16, 512], BF16)
                Wga = A.sb("Wga", [128, 16, 512], BF16)
                hTb = [A.sb(f"hTb{i}", [128, 4, 16, 128], BF16) for i in range(2)]
                cw = A.sb("cw", [128, 4, 4], F32)
                cb = A.sb("cb", [128, 4], F32)
                mgbc = A.sb("mgbc", [128, 512], F32)
                raw = [A.sb(f"raw{i}", [128, 3 + 512], F32) for i in range(4)]
                acc2 = [A.sb(f"acc{i}", [128, 512], F32) for i in range(2)]
                qT = A.sb("qT", [128, 2, 512], BF16)
                kT = A.sb("kT", [128, 2, 512], BF16)
                vt = A.sb("vt", [128, 4, 512], BF16)
                gate = A.sb("gate", [128, 4, 512], F32)
                gtmp = [A.sb(f"gtmp{i}", [128, 512], F32) for i in range(2)]
                kp = A.sb("kp", [128, 4, 256], BF16)
                pTs = A.sb("pTs", [128, 4, 128], BF16)
                Cst = A.sb("Cst", [128, 2, 512], F32)
                Cb = A.sb("Cb", [128, 2, 512], BF16)
                nst = A.sb("nst", [128, 2], F32)
                nb = A.sb("nb", [128, 2], BF16)
                dd = A.sb("dd", [128, 1], F32)
                hmS = A.sb("hmS", [128, 512], F32)
                hjunk = A.sb("hjunk", [128, 512], F32)
                hss = A.sb("hss", [128, 1], F32)
                ybl = A.sb("ybl", [128, 512], F32)
                ysb4 = [A.sb(f"ysb{i}", [128, 512], BF16) for i in range(4)]
                yTs = A.sb("yTs", [128, 4, 128], BF16)
                pwu = A.sb("pwu", [128, 16, 256], BF16)
                pwd = A.sb("pwd", [128, D], BF16)
                pq = [A.ps(f"pq{i}", [128, 4, 128]) for i in range(2)]
                pv = A.ps("pv")
                psd = A.ps("psd")
                pnum = A.ps("pnum")
                pdc = [A.ps(f"pdc{i}") for i in range(2)]
                vbanks = [(pv, "pv"), (pnum, "pnum"), (pdc[0], ("pdc", 0)), (pdc[1], ("pdc", 1))]
                vrot = [0]
                ptr = A.ps("ptr", [128, 1024], BF16)
                wstg = [A.sb(f"wstg{i}", [128, 16, 256], F32) for i in range(2)]
                wl = [(Wk, "Wk", 0, 1024 + hm * 256), (Wv, "Wv", 0, OFF_AV + hm * 512), (Wv, "Wv", 256, OFF_AV + hm * 512 + 256), (Wq, "Wq", 0, hm * 256),
                      (Wo, "Wo", 0, OFF_AO + hm * 512), (Wo, "Wo", 256, OFF_AO + hm * 512 + 256), (Wga, "Wga", 0, OFF_GA + hm * 512), (Wga, "Wga", 256, OFF_GA + hm * 512 + 256)]
                for i, (wt, wkey, dc, sc) in enumerate(wl):
                    P.add("sp", lambda h, i=i, sc=sc: h.dma_start(out=wstg[i % 2][:], in_=wcols(w_in, sc, 256)), writes=[("wstg", i % 2)], dma=True)
                    P.add("pool", lambda h, i=i, wt=wt, dc=dc: h.tensor_copy(out=wt[:, :, dc:dc + 256], in_=wstg[i % 2][:]), reads=[("wstg", i % 2)], writes=[wkey])
                for j in range(4):
                    ch0 = (0 if j < 2 else 1024) + hm * 256 + (j % 2) * 128
                    P.add("sp", lambda h, j=j, ch0=ch0: h.dma_start(out=cw[:, j, :], in_=qk_conv_wT[ch0:ch0 + 128, :]), writes=[("cw", j)], dma=True, sem_key="cwl")
                    P.add("sp", lambda h, j=j, ch0=ch0: h.dma_start(out=cb[:, j:j + 1], in_=qk_conv_b[ch0:ch0 + 128].rearrange("(p o) -> p o", o=1)),
                          writes=[("cb", j)], dma=True, sem_key="cwl")
                P.add("sp", lambda h, hm=hm: h.dma_start(out=mgbc[:], in_=mlstm_norm_g[hm * 512:(hm + 1) * 512].partition_broadcast(128)), writes=["mgbc"], dma=True)
                P.add("dve", lambda h: h.memset(Cst[:], 0.0), writes=["Cst"])
                P.add("dve", lambda h: h.memset(nst[:], 0.0), writes=["nst"])
                P.add("dve", lambda h: h.memset(Cb[:], 0.0), writes=["Cb"])
                P.add("dve", lambda h: h.memset(nb[:], 0.0), writes=["nb"])
                for j in range(4):
                    P.add("dve", lambda h, j=j: h.memset(raw[j][:, 0:3], 0.0), writes=[("raw", j)])

                def load_blk(bi):
                    b0, b1 = blocks[bi]
                    s = bi % 2
                    P.add("sp",
                          lambda h, b0=b0, b1=b1, s=s: h.dma_start(out=hTb[s][:, 0:b1 - b0], in_=hT_d[b0:b1].rearrange("t p c n -> p t c n")),
                          writes=[("hTb", s)], dma=True)

                def do_mblock(bi, b0, b1):
                    nt = b1 - b0
                    n = nt * 128
                    s = bi % 2
                    if bi + 1 < len(blocks):
                        load_blk(bi + 1)
                    hk = [("hTb", s)]
                    own_blk = b1 > NP
                    js = [j for j in range(4) if not (j < 2 and not own_blk)]

                    def conv_head(j):
                        W = Wq if j < 2 else Wk
                        c = j % 2
                        pj = pq[j % 2]
                        ac = acc2[j % 2]
                        for kc in range(16):
                            P.add("pe", lambda h, kc=kc: h.matmul(pj[:, 0:nt, :], lhsT=W[:, kc, c * 128:(c + 1) * 128], rhs=hTb[s][:, 0:nt, kc, :],
                                                                  start=(kc == 0), stop=(kc == 15)), reads=hk + ["Wq", "Wk"], writes=[("pq", j % 2)])
                        P.add("act", lambda h: h.activation(out=raw[j][:, 3:3 + n].rearrange("p (a b) -> p a b", b=128), in_=pj[:, 0:nt, :], func=AF.Copy),
                              reads=[("pq", j % 2)], writes=[("raw", j)])
                        P.add("act", lambda h: h.activation(out=ac[:, 0:n], in_=raw[j][:, 3:3 + n], func=AF.Identity, scale=cw[:, j, 3:4], bias=cb[:, j:j + 1]),
                              reads=[("raw", j), ("cw", j), ("cb", j)], writes=[("acc", j % 2)])
                        for tap in range(3):
                            P.add("dve", lambda h, tap=tap: h.scalar_tensor_tensor(out=ac[:, 0:n], in0=raw[j][:, tap:tap + n], scalar=cw[:, j, tap:tap + 1], in1=ac[:, 0:n],
                                                                               op0=ALU.mult, op1=ALU.add), reads=[("raw", j), ("acc", j % 2)], writes=[("acc", j % 2)])

                    def conv_tail(j):
                        c = j % 2
                        ac = acc2[j % 2]
                        dst = qT if j < 2 else kT
                        P.add("act", lambda h: h.activation(out=dst[:, c, 0:n], in_=ac[:, 0:n], func=AF.Silu), reads=[("acc", j % 2)],
                              writes=[("qT" if j < 2 else "kT", c)])
                        P.add("pool", lambda h: h.tensor_copy(out=raw[j][:, 0:3], in_=raw[j][:, n:n + 3]), reads=[("raw", j)], writes=[("raw", j)])

                    for i, j in enumerate(js):
                        conv_head(j)
                        if i >= 1:
                            conv_tail(js[i - 1])
                    conv_tail(js[-1])
                    def proj_tok(tl, W, wkey):
                        bkt, bkey = vbanks[vrot[0] % 4]
                        vrot[0] += 1
                        for kc in range(16):
                            P.add("pe", lambda h, kc=kc: h.matmul(bkt[:, 0:512], lhsT=hTb[s][:, tl, kc, :], rhs=W[:, kc, :], start=(kc == 0), stop=(kc == 15)),
                                  reads=hk + [wkey], writes=[bkey])
                        return bkt, bkey

                    for t in range(b0, b1):
                        tl = t - b0
                        bkt, bkey = proj_tok(tl, Wv, "Wv")
                        if t < NP:
                            P.add("act", lambda h, tl=tl, bkt=bkt: h.activation(out=vt[:, tl, :], in_=bkt[:, 0:512], func=AF.Copy, scale=keepc[:]), reads=[bkey], writes=[("vt", tl)])
                        else:
                            P.add("dve", lambda h, tl=tl, bkt=bkt: h.tensor_copy(out=vt[:, tl, :], in_=bkt[:, 0:512]), reads=[bkey], writes=[("vt", tl)])
                        if t >= NP:
                            bkt, bkey = proj_tok(tl, Wo, "Wo")
                            P.add("act", lambda h, tl=tl, bkt=bkt: h.activation(out=gate[:, tl, :], in_=bkt[:, 0:512], func=AF.Sigmoid), reads=[bkey], writes=[("gate", tl)])
                            bkt, bkey = proj_tok(tl, Wga, "Wga")
                            gs = tl % 2
                            P.add("act", lambda h, bkt=bkt, gs=gs: h.activation(out=gtmp[gs][:], in_=bkt[:, 0:512], func=AF.Sigmoid), reads=[bkey], writes=[("gtmp", gs)])
                            P.add("pool", lambda h, tl=tl, gs=gs: h.tensor_tensor(out=gate[:, tl, :], in0=gate[:, tl, :], in1=gtmp[gs][:], op=ALU.mult),
                                  reads=[("gate", tl), ("gtmp", gs)], writes=[("gate", tl)])
                    for half in range(0, nt, 2):
                        tls = list(range(half, min(half + 2, nt)))
                        for tl in tls:
                            for c in range(2):
                                o0 = (tl - half) * 256 + c * 128
                                P.add("pe", lambda h, c=c, tl=tl, o0=o0: h.transpose(out=ptr[:, o0:o0 + 128], in_=kT[:, c, tl * 128:(tl + 1) * 128], identity=ident_b[:]),
                                      reads=[("kT", c)], writes=["ptr"])
                        for tl in tls:
                            wcol = WT[:, hm, b0 + tl:b0 + tl + 1]
                            o0 = (tl - half) * 256
                            P.add("act", lambda h, tl=tl, o0=o0, wcol=wcol: h.activation(out=kp[:, tl, :], in_=ptr[:, o0:o0 + 256], func=AF.Copy, scale=wcol),
                                  reads=["ptr"], writes=[("kp", tl)])
                    own_tls = [tl for tl in range(nt) if b0 + tl >= NP]
                    for tl in own_tls:
                        for c in range(2):
                            P.add("pe", lambda h, c=c, tl=tl: h.matmul(psd[:, tl * 128:(tl + 1) * 128], lhsT=kT[:, c, tl * 128:(tl + 1) * 128], rhs=qT[:, c, tl * 128:(tl + 1) * 128],
                                                                   start=(c == 0), stop=(c == 1)), reads=[("kT", c), ("qT", c)], writes=["psd"])
                    for tl in own_tls:
                        wcol = WT[:, hm, b0 + tl:b0 + tl + 1]
                        P.add("dve", lambda h, tl=tl, wcol=wcol: h.scalar_tensor_tensor(out=pTs[:, tl, :], in0=psd[:, tl * 128:(tl + 1) * 128], scalar=wcol, in1=maskU[:],
                                                                                    op0=ALU.mult, op1=ALU.mult), reads=["psd"], writes=[("pTs", tl)])
                    for t in range(b0, b1):
                        tl = t - b0
                        to = t - NP
                        own = t >= NP
                        dcol = DEL[:, hm, t:t + 1]
                        kcol = onecb if own else keepcb
                        for c in range(2):
                            P.add("pe", lambda h, c=c, tl=tl: h.matmul(pdc[c][:, 0:512], lhsT=kp[:, tl, c * 128:(c + 1) * 128], rhs=vt[:, tl, :], start=True, stop=True),
                                  reads=[("kp", tl), ("vt", tl)], writes=[("pdc", c)])
                        for c in range(2):
                            P.add("pe", lambda h, c=c, tl=tl, kcol=kcol: h.matmul(pv[:, 8 + c:9 + c], lhsT=kp[:, tl, c * 128:(c + 1) * 128], rhs=kcol[:], start=True, stop=True),
                                  reads=[("kp", tl)], writes=["pv"])
                        if own:
                            P.add("pe", lambda h, tl=tl: h.matmul(pnum[:, 0:512], lhsT=pTs[:, tl, :], rhs=vt[:, tl, :], start=True, stop=False), reads=[("pTs", tl), ("vt", tl)], writes=["pnum"])
                            for c in range(2):
                                P.add("pe", lambda h, c=c, tl=tl: h.matmul(pnum[:, 0:512], lhsT=qT[:, c, tl * 128:(tl + 1) * 128], rhs=Cb[:, c, :], start=False, stop=(c == 1)),
                                      reads=[("qT", c), "Cb"], writes=["pnum"])
                            P.add("pe", lambda h, tl=tl: h.matmul(pv[:, 0:1], lhsT=pTs[:, tl, :], rhs=onecb[:], start=True, stop=False), reads=[("pTs", tl)], writes=["pv"])
                            for c in range(2):
                                P.add("pe", lambda h, c=c, tl=tl: h.matmul(pv[:, 0:1], lhsT=qT[:, c, tl * 128:(tl + 1) * 128], rhs=nb[:, c:c + 1], start=False, stop=(c == 1)),
                                      reads=[("qT", c), "nb"], writes=["pv"])
                        for c in range(2):
                            P.add("dve", lambda h, c=c, dcol=dcol: h.scalar_tensor_tensor(out=Cst[:, c, :], in0=Cst[:, c, :], scalar=dcol, in1=pdc[c][:, 0:512], op0=ALU.mult, op1=ALU.add),
                                  reads=["Cst", ("pdc", c)], writes=["Cst"])
                        P.add("dve", lambda h, dcol=dcol: h.scalar_tensor_tensor(out=nst[:], in0=nst[:], scalar=dcol, in1=pv[:, 8:10], op0=ALU.mult, op1=ALU.add),
                              reads=["nst", "pv"], writes=["nst"])
                        if own:
                            P.add("act", lambda h: h.activation(out=dd[:], in_=pv[:, 0:1], func=AF.Abs), reads=["pv"], writes=["dd"])
                        if t + 1 < NT and t + 1 >= NP:
                            ncol = DEL[:, hm, t + 1:t + 2]
                            P.add("act", lambda h, ncol=ncol: h.activation(out=Cb[:].rearrange("p a b -> p (a b)"), in_=Cst[:].rearrange("p a b -> p (a b)"), func=AF.Copy, scale=ncol),
                                  reads=["Cst"], writes=["Cb"])
                            P.add("act", lambda h, ncol=ncol: h.activation(out=nb[:], in_=nst[:], func=AF.Copy, scale=ncol), reads=["nst"], writes=["nb"])
                        if own:
                            ccol = CL[:, hm, t:t + 1]
                            P.add("dve", lambda h, ccol=ccol: h.tensor_tensor(out=dd[:], in0=dd[:], in1=ccol, op=ALU.max), reads=["dd"], writes=["dd"])
                            P.add("dve", lambda h: h.reciprocal(out=dd[:], in_=dd[:]), reads=["dd"], writes=["dd"])
                            P.add("act", lambda h: h.activation(out=hmS[:], in_=pnum[:, 0:512], func=AF.Copy, scale=dd[:]), reads=["pnum", "dd"], writes=["hmS"])
                            P.add("act", lambda h: h.activation(out=hjunk[:], in_=hmS[:], func=AF.Square, accum_out=hss[:]), reads=["hmS"], writes=["hjunk", "hss"])
                            P.add("act", lambda h: h.activation(out=hss[:], in_=hss[:], func=AF.Sqrt, scale=1.0 / 512, bias=epsc[:]), reads=["hss"], writes=["hss"])
                            P.add("dve", lambda h: h.reciprocal(out=hss[:], in_=hss[:]), reads=["hss"], writes=["hss"])
                            P.add("sp", lambda h, to=to, hm=hm: h.dma_start(out=ybl[:], in_=yb_d[to * 128:(to + 1) * 128, hm * 512:(hm + 1) * 512]), writes=["ybl"], dma=True)
                            P.add("dve", lambda h: h.scalar_tensor_tensor(out=hmS[:], in0=hmS[:], scalar=hss[:], in1=mgbc[:], op0=ALU.mult, op1=ALU.mult),
                                  reads=["hmS", "hss", "mgbc"], writes=["hmS"])
                            P.add("pool", lambda h, tl=tl: h.tensor_tensor(out=hmS[:], in0=hmS[:], in1=gate[:, tl, :], op=ALU.mult), reads=["hmS", ("gate", tl)], writes=["hmS"])
                            P.add("pool", lambda h, tl=tl: h.tensor_tensor(out=ysb4[tl][:], in0=hmS[:], in1=ybl[:], op=ALU.add), reads=["hmS", "ybl"], writes=[("ysb", tl)])
                            if debug:
                                P.add("dve", lambda h: h.tensor_tensor(out=hjunk[:], in0=hmS[:], in1=ybl[:], op=ALU.add), reads=["hmS", "ybl", "hjunk"], writes=["hjunk"])
                                P.add("sp", lambda h, to=to, hm=hm: h.dma_start(out=dbg["y"][to * 128:(to + 1) * 128, hm * 512:(hm + 1) * 512], in_=hjunk[:]),
                                      reads=["hjunk"], writes=[("dbgy", to)], dma=True, sem_key="dbgy")
                    for t in range(max(b0, NP), b1):
                        tl = t - b0
                        to = t - NP
                        for c in range(4):
                            P.add("pe", lambda h, c=c, tl=tl: h.transpose(out=ptr[:, 512 + c * 128:512 + (c + 1) * 128], in_=ysb4[tl][:, c * 128:(c + 1) * 128], identity=ident_b[:]),
                                  reads=[("ysb", tl)], writes=["ptr"])
                        P.add("act", lambda h: h.activation(out=yTs[:].rearrange("p a b -> p (a b)"), in_=ptr[:, 512:1024], func=AF.Copy), reads=["ptr"], writes=["yTs"])
                        P.add("sp", lambda h, to=to, hm=hm: h.dma_start(out=yT_d[to, :, hm * 4:(hm + 1) * 4, :], in_=yTs[:]), reads=["yTs"], writes=[("yT_d", to)], dma=True,
                              sem_key="yTst")
                pw_j = list(range(hm * (NFF // 4), (hm + 1) * (NFF // 4)))
                pw_per_blk = -(-len(pw_j) // len(blocks))
                pw_state = {"next": 0, "pending": None}

                def pw_step():
                    if pw_state["pending"] is not None:
                        j = pw_state["pending"]
                        P.add("sp", lambda h, j=j: h.dma_start(out=wup_d[j], in_=pwu[:]), reads=["pwu"], writes=[("wup_d", j)], dma=True, sem_key="pwus")
                        P.add("sp", lambda h, j=j: h.dma_start(out=wdn_d[j], in_=pwd[:]), reads=["pwd"], writes=[("wdn_d", j)], dma=True, sem_key="pwds")
                        pw_state["pending"] = None
                    if pw_state["next"] < len(pw_j):
                        j = pw_j[pw_state["next"]]
                        pw_state["next"] += 1
                        pwdf = wstg[1][:].rearrange("p a b -> p (a b)")[:, 0:D]
                        P.add("sp", lambda h, j=j: h.dma_start(out=wstg[0][:, :, 0:128], in_=wcols(w_up, j * 128, 128)), writes=[("wstg", 0)], dma=True, sem_key="pwl0")
                        P.add("sp", lambda h, j=j: h.dma_start(out=wstg[0][:, :, 128:256], in_=wcols(w_up, DFF + j * 128, 128)), writes=[("wstg", 0)], dma=True, sem_key="pwl1")
                        P.add("sp", lambda h, j=j, pwdf=pwdf: h.dma_start(out=pwdf, in_=w_down[j * 128:(j + 1) * 128, :]), writes=[("wstg", 1)], dma=True, sem_key="pwl2")
                        P.add("pool", lambda h: h.tensor_copy(out=pwu[:], in_=wstg[0][:]), reads=[("wstg", 0)], writes=["pwu"])
                        P.add("pool", lambda h, pwdf=pwdf: h.tensor_copy(out=pwd[:], in_=pwdf), reads=[("wstg", 1)], writes=["pwd"])
                        pw_state["pending"] = j

                load_blk(0)
                for bi, (b0_, b1_) in enumerate(blocks):
                    for _ in range(pw_per_blk):
                        pw_step()
                    do_mblock(bi, b0_, b1_)
                pw_step()
                assert pw_state["next"] == len(pw_j) and pw_state["pending"] is None
                P.emit()

        with contextlib.ExitStack() as es:
            A = Alloc(nc, es, "p2a")
            P = Prog(ctx)
            Wout = A.sb("Wout", [128, 16, D], BF16)
            g2bc = A.sb("g2bc", [128, D], F32)
            yTt = [A.sb(f"yTt{i}", [128, 16, 128], BF16) for i in range(2)]
            xt = [A.sb(f"xt{i}", [128, D], F32) for i in range(2)]
            xm = [A.sb(f"xm{i}", [128, D], F32) for i in range(2)]
            junk = A.sb("junk", [128, D], BF16)
            hn2 = [A.sb(f"hn{i}", [128, D], BF16) for i in range(2)]
            ss2 = [A.sb(f"ss{i}", [128, 1], F32) for i in range(2)]
            h2T = [A.sb(f"h2T{i}", [128, 16, 128], BF16) for i in range(2)]
            po = [A.ps(f"po{i}") for i in range(4)]
            tp = [A.ps(f"tp{i}", [128, 8, 128], BF16) for i in range(2)]
            wstg = [A.sb(f"wstg{i}", [128, 16, 256], F32) for i in range(2)]
            for i in range(8):
                P.add("sp", lambda h, i=i: h.dma_start(out=wstg[i % 2][:], in_=wcols(w_out, i * 256, 256)), writes=[("wstg", i % 2)], dma=True)
                P.add("pool", lambda h, i=i: h.tensor_copy(out=Wout[:, :, i * 256:(i + 1) * 256], in_=wstg[i % 2][:]), reads=[("wstg", i % 2)], writes=[("Wout", i // 2)])
            P.add("sp", lambda h: h.dma_start(out=g2bc[:], in_=norm2_g.partition_broadcast(128)), writes=["g2bc"], dma=True)
            def loads_2a(to):
                P.add("sp", lambda h: h.dma_start(out=yTt[to % 2][:], in_=yT_d[to]), writes=[("yTt", to % 2)], dma=True)
                P.add("sp", lambda h: h.dma_start(out=xt[to % 2][:], in_=xin[(to + NP) * 128:(to + NP + 1) * 128, :]), writes=[("xt", to % 2)], dma=True)

            def stage_a(to):
                t = to + NP
                s = to % 2
                if to == 0:
                    loads_2a(0)
                if to + 1 < NO:
                    loads_2a(to + 1)
                for q4 in range(4):
                    for kc in range(16):
                        P.add("pe", lambda h, q4=q4, kc=kc: h.matmul(po[q4][:, 0:512], lhsT=yTt[s][:, kc, :], rhs=Wout[:, kc, q4 * 512:(q4 + 1) * 512], start=(kc == 0), stop=(kc == 15)),
                              reads=[("yTt", s), ("Wout", q4)], writes=[("po", q4)])
                    P.add("dve", lambda h, q4=q4: h.tensor_tensor(out=xm[s][:, q4 * 512:(q4 + 1) * 512], in0=xt[s][:, q4 * 512:(q4 + 1) * 512], in1=po[q4][:, 0:512], op=ALU.add),
                          reads=[("xt", s), ("po", q4)], writes=[("xm", s, q4)])
                xk = [("xm", s, q4) for q4 in range(4)]
                P.add("sp", lambda h: h.dma_start(out=xm_d[to * 128:(to + 1) * 128, :], in_=xm[s][:]), reads=xk, writes=[("xm_d", to)], dma=True, sem_key=("xmst", s))
                if debug:
                    P.add("sp", lambda h: h.dma_start(out=dbg["xm"][to * 128:(to + 1) * 128, :], in_=xm[s][:]), reads=xk, writes=[("dbgxm", to)], dma=True, sem_key=("dbgxm", s))
                P.add("act", lambda h: h.activation(out=junk[:], in_=xm[s][:], func=AF.Square, accum_out=ss2[s][:]), reads=xk, writes=["junk", ("ss", s)])
                P.add("act", lambda h: h.activation(out=ss2[s][:], in_=ss2[s][:], func=AF.Sqrt, scale=1.0 / D, bias=epsc[:]), reads=[("ss", s)], writes=[("ss", s)])
                P.add("dve", lambda h: h.reciprocal(out=ss2[s][:], in_=ss2[s][:]), reads=[("ss", s)], writes=[("ss", s)])
                P.add("dve", lambda h: h.scalar_tensor_tensor(out=hn2[s][:], in0=xm[s][:], scalar=ss2[s][:], in1=g2bc[:], op0=ALU.mult, op1=ALU.mult),
                      reads=xk + [("ss", s), "g2bc"], writes=[("hn", s)])

            def stage_b(to):
                s = to % 2
                for half in range(2):
                    for c in range(8):
                        cc = half * 8 + c
                        P.add("pe", lambda h, half=half, c=c, cc=cc: h.transpose(out=tp[half][:, c, :], in_=hn2[s][:, cc * 128:(cc + 1) * 128], identity=ident_b[:]),
                              reads=[("hn", s)], writes=[("tp", half)])
                    if half == 0:
                        P.add("act", lambda h: h.activation(out=h2T[s][:, 0:8, :], in_=tp[0][:], func=AF.Copy), reads=[("tp", 0)], writes=[("h2T", s, 0)])
                    else:
                        P.add("dve", lambda h: h.tensor_copy(out=h2T[s][:, 8:16, :], in_=tp[1][:]), reads=[("tp", 1)], writes=[("h2T", s, 1)])
                P.add("sp", lambda h: h.dma_start(out=h2T_d[to], in_=h2T[s][:]), reads=[("h2T", s, 0), ("h2T", s, 1)], writes=[("h2T_d", to)], dma=True, sem_key=("h2st", s))

            stage_a(0)
            for to in range(NO):
                if to + 1 < NO:
                    stage_a(to + 1)
                stage_b(to)
            P.emit()

        oblocks = [(b0, min(b0 + 4, NO)) for b0 in range(0, NO, 4)]
        GRP = 4
        with contextlib.ExitStack() as es:
            A = Alloc(nc, es, "p2b")
            P = Prog(ctx)
            h2b = [A.sb(f"h2b{i}", [128, 4, 16, 128], BF16) for i in range(2)]
            wus = [A.sb(f"wus{i}", [128, 16, 256], BF16) for i in range(3)]
            wds = [A.sb(f"wds{i}", [128, GRP, D], BF16) for i in range(2)]
            fcw = A.sb("fcw", [128, 2 * NFF, 3], F32)
            fcb = A.sb("fcb", [128, 2 * NFF], F32)
            halo = A.sb("halo", [128, 2 * NFF, 2], F32)
            rawg = [A.sb(f"rawg{i}", [128, 2 + 512], F32) for i in range(2)]
            accg = [A.sb(f"accg{i}", [128, 512], F32) for i in range(2)]
            sg = A.sb("sg", [128, 512], F32)
            aT = [A.sb(f"aT{i}", [128, GRP, 512], BF16) for i in range(2)]
            oacc2 = [A.sb(f"oacc{i}", [128, 4, D], F32) for i in range(2)]
            xmt = [A.sb(f"xmt{i}", [128, D], F32) for i in range(2)]
            pu = [A.ps(f"pu{i}", [128, 4, 128]) for i in range(4)]
            pd = [A.ps(f"pd{i}") for i in range(4)]
            P.add("sp", lambda h: h.dma_start(out=fcw[:], in_=ffn_conv_wP[:, :, :]), writes=["fcw"], dma=True)
            P.add("sp", lambda h: h.dma_start(out=fcb[:], in_=ffn_conv_bP[:, :]), writes=["fcb"], dma=True)
            P.add("dve", lambda h: h.memset(halo[:], 0.0), writes=["halo"])
            ngrp = NFF // GRP
            ucount = 0
            dcount = 0
            def h2b_load(bi):
                b0, b1 = oblocks[bi]
                P.add("sp", lambda h: h.dma_start(out=h2b[bi % 2][:, 0:b1 - b0], in_=h2T_d[b0:b1].rearrange("t p c n -> p t c n")), writes=[("h2b", bi % 2)], dma=True)

            def final_block(bi):
                b0, b1 = oblocks[bi]
                oa = oacc2[bi % 2]
                for tl in range(b1 - b0):
                    to = b0 + tl
                    xs = tl % 2
                    ok = [("oacc", bi % 2, tl, q4) for q4 in range(4)]
                    P.add("sp", lambda h, to=to, xs=xs: h.dma_start(out=xmt[xs][:], in_=xm_d[to * 128:(to + 1) * 128, :]), writes=[("xmt", xs)], dma=True)
                    P.add("pool", lambda h, tl=tl, xs=xs: h.tensor_tensor(out=oa[:, tl, :], in0=oa[:, tl, :], in1=xmt[xs][:], op=ALU.add), reads=ok + [("xmt", xs)], writes=ok)
                    P.add("sp", lambda h, tl=tl, to=to: h.dma_start(out=out[to * 128:(to + 1) * 128, :], in_=oa[:, tl, :]),
                          reads=ok, writes=[("out", to)], dma=True, sem_key=("ost", tl % 2))

            h2b_load(0)
            for bi, (b0, b1) in enumerate(oblocks):
                nt = b1 - b0
                n = nt * 128
                s = bi % 2
                oa = oacc2[bi % 2]
                ob = bi % 2
                if bi + 1 < len(oblocks):
                    h2b_load(bi + 1)
                for gi in range(ngrp):
                    if gi == 1 and bi > 0:
                        final_block(bi - 1)
                    ga = gi % 2
                    dsl = dcount % 2
                    dcount += 1
                    P.add("sp", lambda h, gi=gi, dsl=dsl: h.dma_start(out=wds[dsl][:], in_=wdn_d[gi * GRP:(gi + 1) * GRP].rearrange("j p n -> p j n")), writes=[("wds", dsl)], dma=True)
                    for jj in range(GRP):
                        j = gi * GRP + jj
                        us = ucount % 3
                        ucount += 1
                        P.add("sp", lambda h, j=j, us=us: h.dma_start(out=wus[us][:], in_=wup_d[j]), writes=[("wus", us)], dma=True)
                        for gv in range(2):
                            ch = gv * NFF + j
                            pj = pu[(2 * jj + gv) % 4]
                            pkey = ("pu", (2 * jj + gv) % 4)
                            for kc in range(16):
                                P.add("pe", lambda h, s=s, gv=gv, kc=kc, pj=pj, us=us, nt=nt: h.matmul(pj[:, 0:nt, :], lhsT=wus[us][:, kc, gv * 128:(gv + 1) * 128], rhs=h2b[s][:, 0:nt, kc, :],
                                                                                                 start=(kc == 0), stop=(kc == 15)), reads=[("h2b", s), ("wus", us)], writes=[pkey])
                            rg = rawg[gv]
                            P.add("pool", lambda h, rg=rg, ch=ch: h.tensor_copy(out=rg[:, 0:2], in_=halo[:, ch, :]), reads=["halo"], writes=[("rawg", gv)])
                            P.add("act", lambda h, rg=rg, pj=pj, nt=nt, n=n: h.activation(out=rg[:, 2:2 + n].rearrange("p (a b) -> p a b", b=128), in_=pj[:, 0:nt, :], func=AF.Copy),
                                  reads=[pkey], writes=[("rawg", gv)])
                            P.add("act", lambda h, rg=rg, gv=gv, ch=ch, n=n: h.activation(out=accg[gv][:, 0:n], in_=rg[:, 2:2 + n], func=AF.Identity, scale=fcw[:, ch, 2:3], bias=fcb[:, ch:ch + 1]),
                                  reads=[("rawg", gv), "fcw", "fcb"], writes=[("accg", gv)])
                            for tap in range(2):
                                P.add("dve", lambda h, rg=rg, gv=gv, ch=ch, n=n, tap=tap: h.scalar_tensor_tensor(out=accg[gv][:, 0:n], in0=rg[:, tap:tap + n], scalar=fcw[:, ch, tap:tap + 1],
                                                                                                       in1=accg[gv][:, 0:n], op0=ALU.mult, op1=ALU.add),
                                      reads=[("rawg", gv), ("accg", gv)], writes=[("accg", gv)])
                            P.add("pool", lambda h, rg=rg, ch=ch, n=n: h.tensor_copy(out=halo[:, ch, :], in_=rg[:, n:n + 2]), reads=[("rawg", gv)], writes=["halo"])
                        P.add("act", lambda h, n=n: h.activation(out=sg[:, 0:n], in_=accg[0][:, 0:n], func=AF.Silu), reads=[("accg", 0)], writes=["sg"])
                        P.add("dve", lambda h, ga=ga, jj=jj, n=n: h.tensor_tensor(out=aT[ga][:, jj, 0:n], in0=sg[:, 0:n], in1=accg[1][:, 0:n], op=ALU.mult),
                              reads=["sg", ("accg", 1)], writes=[("aT", ga, jj)])
                    for tl in range(nt):
                        for q4 in range(4):
                            pdi = (tl * 4 + q4) % 4
                            for jj in range(GRP):
                                P.add("pe", lambda h, ga=ga, jj=jj, tl=tl, q4=q4, pdi=pdi, dsl=dsl: h.matmul(pd[pdi][:, 0:512], lhsT=aT[ga][:, jj, tl * 128:(tl + 1) * 128],
                                                                                                     rhs=wds[dsl][:, jj, q4 * 512:(q4 + 1) * 512], start=(jj == 0), stop=(jj == GRP - 1)),
                                      reads=[("aT", ga, jj), ("wds", dsl)], writes=[("pd", pdi)])
                            if gi == 0:
                                P.add("dve", lambda h, tl=tl, q4=q4, pdi=pdi, oa=oa: h.tensor_copy(out=oa[:, tl, q4 * 512:(q4 + 1) * 512], in_=pd[pdi][:, 0:512]),
                                      reads=[("pd", pdi)], writes=[("oacc", ob, tl, q4)])
                            else:
                                P.add("dve", lambda h, tl=tl, q4=q4, pdi=pdi, oa=oa: h.tensor_tensor(out=oa[:, tl, q4 * 512:(q4 + 1) * 512], in0=oa[:, tl, q4 * 512:(q4 + 1) * 512],
                                                                                                    in1=pd[pdi][:, 0:512], op=ALU.add), reads=[("pd", pdi), ("oacc", ob, tl, q4)], writes=[("oacc", ob, tl, q4)])
            final_block(len(oblocks) - 1)
            P.emit()
        print(f"[kernel] ops={ctx.n_ops} waits={ctx.n_waits}")
    return nc


_ALIBI_CACHE = {}


def _consts(NT):
    slopes = 2.0 ** (-8.0 * np.arange(1, 9) / 8)
    p = np.arange(128)[:, None]
    d = np.arange(NT)[None, :]
    tab = np.concatenate([s * (p - 64.0 - 128.0 * d) for s in slopes], axis=1).astype(np.float32)
    e = np.arange(NT + 4)[None, :]
    tabW = np.concatenate([s * (p + 128.0 - 128.0 * e) for s in slopes], axis=1).astype(np.float32)
    return {"c_ident": np.eye(128, dtype=np.float32), "c_maskU": np.triu(np.ones((128, 128), np.float32)), "c_alibi": tab, "c_alibiW": tabW}


def _param_maps(inputs):
    m = {}
    for k in ("norm1_g", "w_in", "if_bias", "qk_conv_b", "mlstm_norm_g", "q_norm_g", "k_norm_g", "subln_g", "w_out", "norm2_g", "w_up", "ffn_conv_b", "w_down"):
        m[k] = np.ascontiguousarray(np.asarray(inputs[k], dtype=np.float32)[0])
    m["diff_lambda"] = np.ascontiguousarray(np.asarray(inputs["diff_lambda"], dtype=np.float32)[0].reshape(512))
    m["qk_conv_wT"] = np.ascontiguousarray(np.asarray(inputs["qk_conv_w"], dtype=np.float32)[0].T)
    fw = np.asarray(inputs["ffn_conv_w"], dtype=np.float32)[0]
    m["ffn_conv_wP"] = np.ascontiguousarray(fw.T.reshape(2 * NFF, 128, 3).transpose(1, 0, 2))
    m["ffn_conv_bP"] = np.ascontiguousarray(m.pop("ffn_conv_b").reshape(2 * NFF, 128).T)
    return m


def run_cores(nc, pm, xins, keeps, NT):
    cm = _consts(NT)
    in_maps = []
    for xi, kp in zip(xins, keeps):
        mm = dict(pm)
        mm.update(cm)
        mm["xin"] = np.ascontiguousarray(xi, dtype=np.float32)
        mm["c_keep"] = np.full((128, 1), kp, np.float32)
        in_maps.append(mm)
    res = run_bass_kernel_spmd(nc, in_maps, core_ids=list(range(len(in_maps))))
    return res.results


def kernel(**inputs):
    NT, NP = 64, 31
    x = np.asarray(inputs["x"], dtype=np.float32)
    B, S, _ = x.shape
    assert B == 4 and S == 8192
    nc = build(NT, NP)
    pm = _param_maps(inputs)
    xins, keeps = [], []
    for b in range(4):
        x0 = np.zeros((NT * 128, D), np.float32)
        x0[NP * 128:] = x[b, 0:(NT - NP) * 128]
        xins.append(x0)
        keeps.append(0.0)
        xins.append(x[b])
        keeps.append(1.0)
    res = run_cores(nc, pm, xins, keeps, NT)
    out = np.empty((B, S, D), np.float32)
    for b in range(4):
        o0 = np.asarray(res[2 * b]["out"])
        o1 = np.asarray(res[2 * b + 1]["out"])
        out[b, 0:4096] = o0[0:4096]
        out[b, 4096:] = o1[128:]
    return out
```

```python
import contextlib
import math
import numpy as np
import concourse.bass as bass
import concourse.mybir as mybir
from concourse.bass_utils import run_bass_kernel_spmd

F32 = mybir.dt.float32
BF16 = mybir.dt.bfloat16
AF = mybir.ActivationFunctionType
ALU = mybir.AluOpType
AX = mybir.AxisListType

D = 2048
DFF = 5632
NFF = DFF // 128
OFF_AV, OFF_AO, OFF_IF, OFF_BQ, OFF_BK, OFF_BV, OFF_GA, OFF_GB = 2048, 4096, 6144, 6152, 8200, 10248, 12296, 14344
PIN = 16392
EPS = 1e-6
LN16 = math.log(16.0)
SAME_ENGINE_SYNC = True
N_DMA_SEMS = 48
PSUM_KEYS = {"tp", "gp", "pB", "pT", "bk", "po", "pl", "pq", "pv", "ptr", "psd", "pnum", "pdc", "pu", "pd"}


class Op:
    __slots__ = ("eng", "fn", "deps", "is_dma", "sem_key", "dma_val", "signals", "sig_val")


class Ctx:
    def __init__(self, nc, es):
        self.nc = nc
        self.engs = ("pe", "act", "dve", "pool", "sp")
        self.eng_sem = {e: es.enter_context(nc.semaphore(f"s_{e}")) for e in self.engs}
        self.eng_cnt = {e: 0 for e in self.engs}
        self.dma_sem = [es.enter_context(nc.semaphore(f"s_dma{i}")) for i in range(N_DMA_SEMS)]
        self.dma_cnt = [0] * N_DMA_SEMS
        self.n_ops = 0
        self.n_waits = 0


class Prog:
    def __init__(self, ctx):
        self.ctx = ctx
        self.ops = []
        self.last_writer = {}
        self.readers = {}
        self.key_slot = {}
        self.slot_cnt = list(ctx.dma_cnt)
        self.dma_last = {}

    def add(self, eng, fn, reads=(), writes=(), dma=False, sem_key=None):
        op = Op()
        op.eng, op.fn, op.is_dma, op.signals, op.sig_val = eng, fn, dma, False, None
        deps = set()
        xr = [k for k in reads if (k if isinstance(k, str) else k[0]) in PSUM_KEYS]
        if xr:
            reads = [k for k in reads if k not in xr]
            writes = list(writes) + [k for k in xr if k not in writes]
        for k in reads:
            w = self.last_writer.get(k)
            if w is not None:
                deps.add(w)
        for k in writes:
            w = self.last_writer.get(k)
            if w is not None:
                deps.add(w)
            rd = self.readers.get(k)
            if rd:
                deps.update(rd[0].values())
                deps.update(rd[1])
        if dma:
            if sem_key is None:
                sem_key = writes[0]
            if sem_key not in self.key_slot:
                assert len(self.key_slot) < N_DMA_SEMS, "too many dma sem keys in phase"
                self.key_slot[sem_key] = len(self.key_slot)
            sl = self.key_slot[sem_key]
            op.sem_key = sl
            prev = self.dma_last.get(sl)
            if prev is not None:
                deps.add(prev)
            self.slot_cnt[sl] += 16
            op.dma_val = self.slot_cnt[sl]
            self.dma_last[sl] = op
        else:
            op.sem_key, op.dma_val = None, None
        deps.discard(op)
        for k in reads:
            rd = self.readers.get(k)
            if rd is None:
                rd = self.readers[k] = ({}, [])
            if dma:
                rd[1].append(op)
            else:
                rd[0][eng] = op
        for k in writes:
            self.last_writer[k] = op
            self.readers[k] = None
        op.deps = deps
        for d in deps:
            if d.is_dma:
                continue
            if d.eng == eng and not dma and (eng == "pe" or not SAME_ENGINE_SYNC):
                continue
            d.signals = True
        self.ops.append(op)
        return op

    def emit(self):
        ctx = self.ctx
        nc = ctx.nc
        per_eng = {e: [op for op in self.ops if op.eng == e] for e in ctx.engs}
        for e in ctx.engs:
            comp = [op for op in per_eng[e] if not op.is_dma]
            if comp:
                comp[-1].signals = True
        cnt = dict(ctx.eng_cnt)
        for op in self.ops:
            if not op.is_dma and op.signals:
                cnt[op.eng] += 1
                op.sig_val = cnt[op.eng]
        final_c = cnt
        final_d = list(self.slot_cnt)
        ctx.n_ops += len(self.ops)

        def run(eng_name, handle):
            seen_c = dict(ctx.eng_cnt)
            seen_d = list(ctx.dma_cnt)
            for op in per_eng[eng_name]:
                need_c, need_d = {}, {}
                for d in op.deps:
                    if d.is_dma:
                        if seen_d[d.sem_key] < d.dma_val:
                            need_d[d.sem_key] = max(need_d.get(d.sem_key, 0), d.dma_val)
                    else:
                        if d.eng == eng_name and not op.is_dma and (eng_name == "pe" or not SAME_ENGINE_SYNC):
                            continue
                        if seen_c[d.eng] < d.sig_val:
                            need_c[d.eng] = max(need_c.get(d.eng, 0), d.sig_val)
                for e, v in need_c.items():
                    handle.wait_ge(ctx.eng_sem[e], v)
                    seen_c[e] = v
                    ctx.n_waits += 1
                for k, v in need_d.items():
                    handle.wait_ge(ctx.dma_sem[k], v)
                    seen_d[k] = v
                    ctx.n_waits += 1
                ins = op.fn(handle)
                if op.is_dma:
                    ins.then_inc(ctx.dma_sem[op.sem_key], 16)
                elif op.signals:
                    ins.then_inc(ctx.eng_sem[op.eng], 1)
            for e in ctx.engs:
                if e != eng_name and seen_c[e] < final_c[e]:
                    handle.wait_ge(ctx.eng_sem[e], final_c[e])
            for k in range(N_DMA_SEMS):
                if seen_d[k] < final_d[k]:
                    handle.wait_ge(ctx.dma_sem[k], final_d[k])

        with nc.Block() as block:
            @block.tensor
            def _(h):
                run("pe", h)

            @block.scalar
            def _(h):
                run("act", h)

            @block.vector
            def _(h):
                run("dve", h)

            @block.gpsimd
            def _(h):
                run("pool", h)

            @block.sync
            def _(h):
                run("sp", h)

        ctx.eng_cnt = final_c
        ctx.dma_cnt = final_d


class Alloc:
    def __init__(self, nc, es, pfx):
        self.nc, self.es, self.pfx = nc, es, pfx

    def sb(self, name, shape, dt):
        return self.es.enter_context(self.nc.sbuf_tensor(f"{self.pfx}_{name}", shape, dt))

    def ps(self, name, shape=(128, 512), dt=F32):
        return self.es.enter_context(self.nc.psum_tensor(f"{self.pfx}_{name}", list(shape), dt))


def wcols(w, c0, n):
    return w[:, c0:c0 + n].rearrange("(c p) n -> p c n", p=128)


def build(NT, NP, debug=False):
    NO = NT - NP
    nc = bass.Bass("TRN2", target_bir_lowering=False)
    din = lambda name, shape: nc.dram_tensor(name, list(shape), F32, kind="ExternalInput").ap()
    xin = din("xin", (NT * 128, D))
    norm1_g = din("norm1_g", (D,))
    w_in = din("w_in", (D, PIN))
    if_bias = din("if_bias", (8,))
    qk_conv_wT = din("qk_conv_wT", (2048, 4))
    qk_conv_b = din("qk_conv_b", (2048,))
    mlstm_norm_g = din("mlstm_norm_g", (2048,))
    q_norm_g = din("q_norm_g", (128,))
    k_norm_g = din("k_norm_g", (128,))
    diff_lambda = din("diff_lambda", (512,))
    subln_g = din("subln_g", (256,))
    w_out = din("w_out", (D, D))
    norm2_g = din("norm2_g", (D,))
    w_up = din("w_up", (D, 2 * DFF))
    ffn_conv_wP = din("ffn_conv_wP", (128, 2 * NFF, 3))
    ffn_conv_bP = din("ffn_conv_bP", (128, 2 * NFF))
    w_down = din("w_down", (DFF, D))
    c_ident = din("c_ident", (128, 128))
    c_maskU = din("c_maskU", (128, 128))
    c_keep = din("c_keep", (128, 1))
    c_alibi = din("c_alibi", (128, 8 * NT))
    c_alibiW = din("c_alibiW", (128, 8 * (NT + 4)))
    out = nc.dram_tensor("out", [NO * 128, D], F32, kind="ExternalOutput").ap()

    hT_d = nc.dram_tensor("hT_d", [NT, 128, 16, 128], BF16).ap()
    yb_d = nc.dram_tensor("yb_d", [NO * 128, D], F32).ap()
    yT_d = nc.dram_tensor("yT_d", [NO, 128, 16, 128], BF16).ap()
    xm_d = nc.dram_tensor("xm_d", [NO * 128, D], F32).ap()
    h2T_d = nc.dram_tensor("h2T_d", [NO, 128, 16, 128], BF16).ap()
    wup_d = nc.dram_tensor("wup_d", [NFF, 128, 16, 256], BF16).ap()
    wdn_d = nc.dram_tensor("wdn_d", [NFF, 128, D], BF16).ap()
    dbg = {}
    if debug:
        dbg["y"] = nc.dram_tensor("dbg_y", [NO * 128, D], F32, kind="ExternalOutput").ap()
        dbg["yb"] = nc.dram_tensor("dbg_yb", [NO * 128, D], F32, kind="ExternalOutput").ap()
        dbg["xm"] = nc.dram_tensor("dbg_xm", [NO * 128, D], F32, kind="ExternalOutput").ap()

    ges = contextlib.ExitStack()
    with ges:
        ctx = Ctx(nc, ges)
        G = Alloc(nc, ges, "g")
        ident_f = G.sb("ident_f", [128, 128], F32)
        ident_b = G.sb("ident_b", [128, 128], BF16)
        maskU = G.sb("maskU", [128, 128], F32)
        ones_f = G.sb("ones_f", [128, 128], F32)
        ones_b = G.sb("ones_b", [128, 128], BF16)
        keepc = G.sb("keepc", [128, 1], F32)
        onec = G.sb("onec", [128, 1], F32)
        onecb = G.sb("onecb", [128, 1], BF16)
        keepcb = G.sb("keepcb", [128, 1], BF16)
        epsc = G.sb("epsc", [128, 1], F32)
        ln16c = G.sb("ln16c", [128, 1], F32)
        alibi = G.sb("alibi", [128, 8 * NT], F32)
        alibiW = G.sb("alibiW", [128, 8 * (NT + 4)], F32)
        gq = G.sb("gq", [128, 1], F32)
        gk = G.sb("gk", [128, 1], F32)
        neglam = G.sb("neglam", [128, 1], F32)
        sublg = G.sb("sublg", [128, 256], F32)
        ifb = G.sb("ifb", [128, 8], F32)
        Graw = G.sb("Graw", [128, NT, 8], F32)
        WT = G.sb("WT", [128, 4, NT], F32)
        CL = G.sb("CL", [128, 4, NT], F32)
        DEL = G.sb("DEL", [128, 4, NT], F32)

        with contextlib.ExitStack() as es:
            A = Alloc(nc, es, "c")
            P = Prog(ctx)
            gbc = A.sb("gbc", [128, 2, 128], F32)
            gmx = A.sb("gmx", [128, 2], F32)
            negM = A.sb("negM", [128, 1], F32)
            lbc = A.sb("lbc", [128, 4, 128], F32)
            lpr = A.sb("lpr", [128, 2, 128], F32)
            lsum = A.sb("lsum", [128, 2], F32)
            P.add("sp", lambda h: h.dma_start(out=ident_f[:], in_=c_ident[:, :]), writes=["ident_f"], dma=True)
            P.add("sp", lambda h: h.dma_start(out=maskU[:], in_=c_maskU[:, :]), writes=["maskU"], dma=True)
            P.add("sp", lambda h: h.dma_start(out=keepc[:], in_=c_keep[:, :]), writes=["keepc"], dma=True)
            P.add("sp", lambda h: h.dma_start(out=alibi[:], in_=c_alibi[:, :]), writes=["alibi"], dma=True)
            P.add("sp", lambda h: h.dma_start(out=alibiW[:], in_=c_alibiW[:, :]), writes=["alibiW"], dma=True)
            P.add("sp", lambda h: h.dma_start(out=gq[:], in_=q_norm_g.rearrange("(p o) -> p o", o=1)), writes=["gq"], dma=True)
            P.add("sp", lambda h: h.dma_start(out=gk[:], in_=k_norm_g.rearrange("(p o) -> p o", o=1)), writes=["gk"], dma=True)
            P.add("sp", lambda h: h.dma_start(out=gbc[:, 0, :], in_=q_norm_g.partition_broadcast(128)), writes=["gbc0"], dma=True)
            P.add("sp", lambda h: h.dma_start(out=gbc[:, 1, :], in_=k_norm_g.partition_broadcast(128)), writes=["gbc1"], dma=True)
            P.add("sp", lambda h: h.dma_start(out=lbc[:].rearrange("p a b -> p (a b)"), in_=diff_lambda.partition_broadcast(128)), writes=["lbc"], dma=True)
            P.add("sp", lambda h: h.dma_start(out=sublg[:], in_=subln_g.partition_broadcast(128)), writes=["sublg"], dma=True)
            P.add("sp", lambda h: h.dma_start(out=ifb[:], in_=if_bias.partition_broadcast(128)), writes=["ifb"], dma=True)
            P.add("dve", lambda h: h.tensor_copy(out=ident_b[:], in_=ident_f[:]), reads=["ident_f"], writes=["ident_b"])
            P.add("dve", lambda h: h.memset(ones_f[:], 1.0), writes=["ones_f"])
            P.add("dve", lambda h: h.memset(ones_b[:], 1.0), writes=["ones_b"])
            P.add("dve", lambda h: h.memset(onec[:], 1.0), writes=["onec"])
            P.add("dve", lambda h: h.memset(onecb[:], 1.0), writes=["onecb"])
            P.add("dve", lambda h: h.memset(epsc[:], EPS), writes=["epsc"])
            P.add("dve", lambda h: h.memset(ln16c[:], LN16), writes=["ln16c"])
            P.add("dve", lambda h: h.tensor_copy(out=keepcb[:], in_=keepc[:]), reads=["keepc"], writes=["keepcb"])
            P.add("dve", lambda h: h.scalar_tensor_tensor(out=gbc[:], in0=gbc[:], scalar=-1.0, in1=gbc[:], op0=ALU.mult, op1=ALU.max), reads=["gbc0", "gbc1"], writes=["gbc0", "gbc1"])
            P.add("dve", lambda h: h.tensor_reduce(out=gmx[:], in_=gbc[:], axis=AX.X, op=ALU.max), reads=["gbc0", "gbc1"], writes=["gmx"])
            P.add("dve", lambda h: h.scalar_tensor_tensor(out=negM[:], in0=gmx[:, 0:1], scalar=-math.sqrt(128.0), in1=gmx[:, 1:2],
                                                         op0=ALU.mult, op1=ALU.mult), reads=["gmx"], writes=["negM"])
            P.add("dve", lambda h: h.tensor_scalar(out=alibi[:], in0=alibi[:], scalar1=negM[:], scalar2=None, op0=ALU.add),
                  reads=["alibi", "negM"], writes=["alibi"])
            P.add("dve", lambda h: h.tensor_scalar(out=alibiW[:], in0=alibiW[:], scalar1=negM[:], scalar2=None, op0=ALU.add),
                  reads=["alibiW", "negM"], writes=["alibiW"])
            P.add("dve", lambda h: h.tensor_tensor(out=lpr[:, 0, :], in0=lbc[:, 0, :], in1=lbc[:, 1, :], op=ALU.mult), reads=["lbc"], writes=["lpr0"])
            P.add("dve", lambda h: h.tensor_tensor(out=lpr[:, 1, :], in0=lbc[:, 2, :], in1=lbc[:, 3, :], op=ALU.mult), reads=["lbc"], writes=["lpr1"])
            P.add("dve", lambda h: h.tensor_reduce(out=lsum[:], in_=lpr[:], axis=AX.X, op=ALU.add), reads=["lpr0", "lpr1"], writes=["lsum"])
            P.add("act", lambda h: h.activation(out=lsum[:], in_=lsum[:], func=AF.Exp), reads=["lsum"], writes=["lsum"])
            P.add("dve", lambda h: h.scalar_tensor_tensor(out=neglam[:], in0=lsum[:, 1:2], scalar=-0.2, in1=lsum[:, 0:1],
                                                         op0=ALU.add, op1=ALU.subtract), reads=["lsum"], writes=["neglam"])
            P.add("dve", lambda h: h.tensor_scalar(out=sublg[:], in0=sublg[:], scalar1=0.8, scalar2=None, op0=ALU.mult), reads=["sublg"], writes=["sublg"])
            P.emit()

        with contextlib.ExitStack() as es:
            A = Alloc(nc, es, "p0")
            P = Prog(ctx)
            xt = [A.sb(f"xt{i}", [128, D], F32) for i in range(2)]
            g1bc = A.sb("g1bc", [128, D], F32)
            hn = [A.sb(f"hn{i}", [128, D], BF16) for i in range(2)]
            ss = [A.sb(f"ss{i}", [128, 1], F32) for i in range(2)]
            rstd = [A.sb(f"rstd{i}", [128, 1], F32) for i in range(2)]
            junk = A.sb("junk", [128, D], BF16)
            hT = [A.sb(f"hT{i}", [128, 16, 128], BF16) for i in range(2)]
            wif = A.sb("wif", [128, 16, 8], BF16)
            tp = [A.ps(f"tp{i}", [128, 8, 128], BF16) for i in range(2)]
            gp = A.ps("gp")
            P.add("sp", lambda h: h.dma_start(out=g1bc[:], in_=norm1_g.partition_broadcast(128)), writes=["g1bc"], dma=True)
            wif_f = A.sb("wif_f", [128, 16, 8], F32)
            P.add("sp", lambda h: h.dma_start(out=wif_f[:], in_=wcols(w_in, OFF_IF, 8)), writes=["wif_f"], dma=True)
            P.add("pool", lambda h: h.tensor_copy(out=wif[:], in_=wif_f[:]), reads=["wif_f"], writes=["wif"])
            def p0_load(t):
                P.add("sp", lambda h, t=t: h.dma_start(out=xt[t % 2][:], in_=xin[t * 128:(t + 1) * 128, :]), writes=[("xt", t % 2)], dma=True)

            p0_load(0)
            for t in range(NT):
                s = t % 2
                if t + 1 < NT:
                    p0_load(t + 1)
                P.add("act", lambda h, s=s: h.activation(out=junk[:], in_=xt[s][:], func=AF.Square, accum_out=ss[s][:]),
                      reads=[("xt", s)], writes=["junk", ("ss", s)])
                P.add("act", lambda h, s=s: h.activation(out=rstd[s][:], in_=ss[s][:], func=AF.Sqrt, scale=1.0 / D, bias=epsc[:]),
                      reads=[("ss", s)], writes=[("rstd", s)])
                P.add("dve", lambda h, s=s: h.reciprocal(out=rstd[s][:], in_=rstd[s][:]), reads=[("rstd", s)], writes=[("rstd", s)])
                P.add("dve", lambda h, s=s: h.scalar_tensor_tensor(out=hn[s][:], in0=xt[s][:], scalar=rstd[s][:], in1=g1bc[:], op0=ALU.mult, op1=ALU.mult),
                      reads=[("xt", s), ("rstd", s), "g1bc"], writes=[("hn", s)])
                for half in range(2):
                    for c in range(8):
                        cc = half * 8 + c
                        P.add("pe", lambda h, s=s, half=half, c=c, cc=cc: h.transpose(out=tp[half][:, c, :], in_=hn[s][:, cc * 128:(cc + 1) * 128], identity=ident_b[:]),
                              reads=[("hn", s)], writes=[("tp", half)])
                    if half == 0:
                        P.add("act", lambda h, s=s: h.activation(out=hT[s][:, 0:8, :], in_=tp[0][:], func=AF.Copy), reads=[("tp", 0)], writes=[("hT", s, 0)])
                    else:
                        P.add("dve", lambda h, s=s: h.tensor_copy(out=hT[s][:, 8:16, :], in_=tp[1][:]), reads=[("tp", 1)], writes=[("hT", s, 1)])
                P.add("sp", lambda h, t=t, s=s: h.dma_start(out=hT_d[t], in_=hT[s][:]), reads=[("hT", s, 0), ("hT", s, 1)], writes=[("hT_d", t)],
                      dma=True, sem_key=("hTst", s))
                for kc in range(16):
                    P.add("pe", lambda h, s=s, kc=kc: h.matmul(gp[:, 0:8], lhsT=hT[s][:, kc, :], rhs=wif[:, kc, :], start=(kc == 0), stop=(kc == 15)),
                          reads=[("hT", s, 0), ("hT", s, 1), "wif"], writes=["gp"])
                P.add("dve", lambda h, t=t: h.tensor_copy(out=Graw[:, t, :], in_=gp[:, 0:8]), reads=["gp"], writes=[("Graw", t)])
            P.emit()

        with contextlib.ExitStack() as es:
            A = Alloc(nc, es, "pg")
            P = Prog(ctx)
            N4 = 4 * NT
            LI = A.sb("LI", [128, 4, NT], F32)
            LF = A.sb("LF", [128, 4, NT], F32)
            Bc = A.sb("Bc", [128, 4, NT], F32)
            BL = A.sb("BL", [128, 4, NT], F32)
            Aa = A.sb("Aa", [128, 4, NT], F32)
            AMX = A.sb("AMX", [128, 4, NT], F32)
            MU = A.sb("MU", [128, 4, NT], F32)
            MST = A.sb("MST", [128, 4, NT + 1], F32)
            TMP = A.sb("TMP", [128, 4, NT], F32)
            negfb = A.sb("negfb", [128, 4], F32)
            amax = A.sb("amax", [128, 1], F32)
            diag = A.sb("diag", [128, 128], F32)
            pB = A.ps("pB")
            pT = A.ps("pT")
            flat = lambda tns: tns[:].rearrange("p a b -> p (a b)")
            P.add("dve", lambda h: h.tensor_scalar(out=negfb[:], in0=ifb[:, 4:8], scalar1=-1.0, scalar2=None, op0=ALU.mult), writes=["negfb"])
            for hd in range(4):
                P.add("dve", lambda h, hd=hd: h.tensor_scalar(out=LI[:, hd, :], in0=Graw[:, :, hd], scalar1=ifb[:, hd:hd + 1], scalar2=None, op0=ALU.add),
                      writes=[("LI", hd)])
                P.add("act", lambda h, hd=hd: h.activation(out=LF[:, hd, :], in_=Graw[:, :, 4 + hd], func=AF.Exp, scale=-1.0, bias=negfb[:, hd:hd + 1]),
                      reads=["negfb"], writes=[("LF", hd)])
            allk = lambda nm: [(nm, hd) for hd in range(4)]
            P.add("act", lambda h: h.activation(out=flat(LF), in_=flat(LF), func=AF.Ln, bias=onec[:]), reads=allk("LF"), writes=allk("LF"))
            P.add("dve", lambda h: h.tensor_scalar(out=flat(LF), in0=flat(LF), scalar1=-1.0, scalar2=None, op0=ALU.mult), reads=allk("LF"), writes=allk("LF"))
            nchunk = (N4 + 511) // 512
            for ci in range(nchunk):
                c0, c1 = ci * 512, min(N4, ci * 512 + 512)
                P.add("pe", lambda h, c0=c0, c1=c1: h.matmul(pB[:, 0:c1 - c0], lhsT=maskU[:], rhs=flat(LF)[:, c0:c1], start=True, stop=True), reads=allk("LF"), writes=["pB"])
                P.add("dve", lambda h, c0=c0, c1=c1: h.tensor_copy(out=flat(Bc)[:, c0:c1], in_=pB[:, 0:c1 - c0]), reads=["pB"], writes=[("Bc", ci)])
                P.add("pe", lambda h, c0=c0, c1=c1: h.matmul(pB[:, 0:c1 - c0], lhsT=ones_f[:], rhs=flat(LF)[:, c0:c1], start=True, stop=True), reads=allk("LF"), writes=["pB"])
                P.add("dve", lambda h, c0=c0, c1=c1: h.tensor_copy(out=flat(BL)[:, c0:c1], in_=pB[:, 0:c1 - c0]), reads=["pB"], writes=[("BL", ci)])
            bk = [("Bc", ci) for ci in range(nchunk)]
            blk = [("BL", ci) for ci in range(nchunk)]
            P.add("dve", lambda h: h.tensor_tensor(out=flat(Aa), in0=flat(LI), in1=flat(Bc), op=ALU.subtract), reads=allk("LI") + bk, writes=["Aa"])
            for c0 in range(0, N4, 128):
                w = min(128, N4 - c0)
                P.add("pe", lambda h, c0=c0, w=w: h.transpose(out=pT[0:w, 0:128], in_=flat(Aa)[:, c0:c0 + w], identity=ident_f[:]), reads=["Aa"], writes=["pT"])
                P.add("dve", lambda h, w=w: h.tensor_reduce(out=amax[0:w, :], in_=pT[0:w, 0:128], axis=AX.X, op=ALU.max), reads=["pT"], writes=["amax"])
                P.add("dve", lambda h, w=w: h.tensor_scalar(out=diag[0:w, 0:w], in0=ident_f[0:w, 0:w], scalar1=amax[0:w, :], scalar2=None, op0=ALU.mult),
                      reads=["amax"], writes=["diag"])
                P.add("pe", lambda h, w=w: h.matmul(pB[:, 0:w], lhsT=ones_f[0:w, :], rhs=diag[0:w, 0:w], start=True, stop=True), reads=["diag"], writes=["pB"])
                P.add("dve", lambda h, c0=c0, w=w: h.tensor_copy(out=flat(AMX)[:, c0:c0 + w], in_=pB[:, 0:w]), reads=["pB"], writes=[("AMX", c0)])
            amk = [("AMX", c0) for c0 in range(0, N4, 128)]
            P.add("dve", lambda h: h.memset(MST[:, :, 0:1], 0.0), writes=["MST"])
            for t in range(NT):
                P.add("dve", lambda h, t=t: h.tensor_tensor(out=MU[:, :, t:t + 1], in0=MST[:, :, t:t + 1], in1=AMX[:, :, t:t + 1], op=ALU.max),
                      reads=["MST"] + amk, writes=["MU"])
                P.add("dve", lambda h, t=t: h.tensor_tensor(out=MST[:, :, t + 1:t + 2], in0=MU[:, :, t:t + 1], in1=BL[:, :, t:t + 1], op=ALU.add),
                      reads=["MU"] + blk, writes=["MST"])
            P.add("dve", lambda h: h.tensor_tensor(out=TMP[:], in0=MST[:, :, 0:NT], in1=MU[:], op=ALU.subtract), reads=["MST", "MU"], writes=["TMP"])
            P.add("act", lambda h: h.activation(out=flat(DEL), in_=flat(TMP), func=AF.Exp), reads=["TMP"], writes=["DEL"])
            P.add("dve", lambda h: h.tensor_tensor(out=flat(TMP), in0=flat(Aa), in1=flat(MU), op=ALU.subtract), reads=["Aa", "MU", "TMP"], writes=["TMP"])
            P.add("act", lambda h: h.activation(out=flat(WT), in_=flat(TMP), func=AF.Exp), reads=["TMP"], writes=["WT"])
            P.add("dve", lambda h: h.tensor_tensor(out=flat(TMP), in0=flat(Bc), in1=flat(MU), op=ALU.add), reads=bk + ["MU", "TMP"], writes=["TMP"])
            P.add("act", lambda h: h.activation(out=flat(CL), in_=flat(TMP), func=AF.Exp, scale=-1.0, bias=ln16c[:]), reads=["TMP"], writes=["CL"])
            P.emit()

        blocks = [(b0, min(b0 + 4, NT)) for b0 in range(0, NT, 4)]
        TSKIP = 100.0
        dstack = contextlib.ExitStack()
        DA = Alloc(nc, dstack, "dw")
        WallD = [DA.sb(f"Wall{i}", [128, 16, 1024], BF16) for i in range(2)]
        wstgD = [DA.sb(f"wstg{i}", [128, 16, 64], F32) for i in range(2)]

        def wchunks(hd_):
            bases = (OFF_BQ + hd_ * 256, OFF_BK + hd_ * 256, OFF_BV + hd_ * 256, OFF_GB + hd_ * 256)
            return [(i * 64, bases[i // 4] + (i % 4) * 64) for i in range(16)]

        for hd in range(8):
            slope = 2.0 ** -(hd + 1)
            wide = slope <= 0.125
            kb = int(math.ceil(TSKIP / (slope * 128.0)))
            q_first = (NP // 4) * 4 if wide else NP
            first_needed = max(0, q_first - kb)
            blocks_h = [blk for blk in blocks if blk[1] > first_needed]
            with contextlib.ExitStack() as es:
                A = Alloc(nc, es, f"d{hd}")
                P = Prog(ctx)
                Wall = WallD[hd % 2]
                hTb = [A.sb(f"hTb{i}", [128, 4, 16, 128], BF16) for i in range(2)]
                kT = A.sb("kT", [128, 2, NT * 128], BF16)
                va = A.sb("va", [128, NT, 258], BF16)
                qT = A.sb("qT", [128, 2, 512], BF16)
                sq = A.sb("sq", [128, 512], BF16)
                rs = A.sb("rs", [128, 512], F32)
                gb = [A.sb(f"gb{i}", [128, 256], F32) for i in range(4)]
                pt = [A.sb(f"pt{i}", [128, 512], BF16) for i in range(3)]
                o_sb = A.sb("o_sb", [128, 256], F32)
                o_junk = A.sb("o_junk", [128, 256], F32)
                yb_sb = [A.sb(f"yb{i}", [128, 256], F32) for i in range(2)]
                r12 = A.sb("r12", [128, 8], F32)
                oss = A.sb("oss", [128, 1], F32)
                bank = [A.ps(f"bk{i}", [128, 4, 128]) for i in range(3)]
                po = [A.ps(f"po{i}") for i in range(4)]
                pl = A.ps("pl")
                Wq, Wk, Wv, Wg = (Wall[:, :, i * 256:(i + 1) * 256] for i in range(4))
                wctr = [0]

                def wstep(hd_, dc, sc, cur):
                    i = wctr[0] % 2
                    wctr[0] += 1
                    dstW = WallD[hd_ % 2]
                    P.add("sp", lambda h: h.dma_start(out=wstgD[i][:], in_=wcols(w_in, sc, 64)), writes=[("wstgD", i)], dma=True)
                    P.add("pool", lambda h: h.tensor_copy(out=dstW[:, :, dc:dc + 64], in_=wstgD[i][:]), reads=[("wstgD", i)],
                          writes=[("W", dc // 256)] if cur else [("Wn", dc)])

                if hd == 0:
                    for dc, sc in wchunks(0):
                        wstep(0, dc, sc, True)
                pf = wchunks(hd + 1) if hd + 1 < 8 else []
                pf_per_blk = -(-len(pf) // len(blocks_h)) if pf else 0
                P.add("dve", lambda h: h.memset(va[:, :, 256:257], 1.0), writes=[("vaone", 0)])
                if NP > 0:
                    P.add("dve", lambda h: h.tensor_scalar(out=va[:, 0:NP, 256:257], in0=va[:, 0:NP, 256:257], scalar1=keepc[:], scalar2=None, op0=ALU.mult),
                          reads=[("vaone", 0)], writes=[("vaone", 0)])

                def load_blk(bi):
                    b0, b1 = blocks_h[bi]
                    s = bi % 2
                    P.add("sp", lambda h, b0=b0, b1=b1, s=s: h.dma_start(out=hTb[s][:, 0:b1 - b0], in_=hT_d[b0:b1].rearrange("t p c n -> p t c n")),
                          writes=[("hTb", s)], dma=True)

                rawsb = [A.sb(f"rawsb{i}", [128, 512], F32) for i in range(2)]
                rot = [0]
                prot = [0]

                def nextbank():
                    rot[0] ^= 2
                    return rot[0]

                def qknorm(bkr, nt, gcol, dst, dkeys, ri):
                    n = nt * 128
                    srcf = bank[bkr][:, 0:nt, :]
                    v3 = lambda ap: ap.rearrange("p (a b) -> p a b", b=128)
                    P.add("act", lambda h: h.activation(out=v3(sq[:, 0:n]), in_=srcf, func=AF.Square), reads=[("bk", bkr)], writes=["sq"])
                    P.add("dve", lambda h: h.tensor_copy(out=v3(rawsb[ri][:, 0:n]), in_=srcf), reads=[("bk", bkr)], writes=[("rawsb", ri)])
                    P.add("pe", lambda h: h.matmul(bank[1][:, 0:nt, :], lhsT=ones_b[:], rhs=v3(sq[:, 0:n]), start=True, stop=True),
                          reads=["sq"], writes=[("bk", 1)])
                    P.add("act", lambda h: h.activation(out=v3(rs[:, 0:n]), in_=bank[1][:, 0:nt, :], func=AF.Sqrt, scale=1.0 / 128, bias=epsc[:]),
                          reads=[("bk", 1)], writes=["rs"])
                    P.add("dve", lambda h: h.reciprocal(out=rs[:, 0:n], in_=rs[:, 0:n]), reads=["rs"], writes=["rs"])
                    P.add("dve", lambda h: h.scalar_tensor_tensor(out=dst, in0=rawsb[ri][:, 0:n], scalar=gcol, in1=rs[:, 0:n], op0=ALU.mult, op1=ALU.mult),
                          reads=[("rawsb", ri), "rs"], writes=dkeys)

                def attention(q0, q1, jmin, qoff):
                    nq = q1 - q0
                    units = []
                    for kt in range(max(0, q0 - kb), q1):
                        jlo = max(jmin, kt - q0, 0)
                        if jlo >= nq:
                            continue
                        for c in range(2):
                            units.append((kt, c, jlo))
                    kt_first = units[0][0]

                    def e_st(ui):
                        kt, c, jlo = units[ui]
                        s3 = ui % 3
                        P.add("pe", lambda h: h.matmul(bank[s3][:, jlo:nq, :], lhsT=kT[:, c, kt * 128:(kt + 1) * 128],
                                                       rhs=qT[:, c, qoff + jlo * 128:qoff + nq * 128].rearrange("p (a b) -> p a b", b=128), start=True, stop=True),
                              reads=[("kT", c, kt), ("qT", c)], writes=[("bk", s3)])

                    def e_act(ui):
                        kt, c, jlo = units[ui]
                        s3 = ui % 3
                        if wide:
                            col = hd * (NT + 4) + (q0 + 3 - kt)
                            bcol = alibiW[:, col:col + 1]
                        else:
                            col = hd * NT + (q0 - kt)
                            bcol = alibi[:, col:col + 1]
                        P.add("act", lambda h: h.activation(out=pt[s3][:, jlo * 128:nq * 128].rearrange("p (a b) -> p a b", b=128), in_=bank[s3][:, jlo:nq, :], func=AF.Exp,
                                                            scale=128.0 ** -0.5, bias=bcol), reads=[("bk", s3)], writes=[("pt", s3)])
                        jd = kt - q0
                        if jd >= jlo:
                            P.add("dve", lambda h: h.tensor_tensor(out=pt[s3][:, jd * 128:(jd + 1) * 128], in0=pt[s3][:, jd * 128:(jd + 1) * 128], in1=maskU[:], op=ALU.mult),
                                  reads=[("pt", s3)], writes=[("pt", s3)])

                    pv_list = [(ui, j) for ui in range(len(units)) for j in range(units[ui][2], nq)]
                    first_b, last_b = {}, {}
                    for idx, (ui, j) in enumerate(pv_list):
                        bk_ = (units[ui][1] * 4 + j) // 2
                        first_b.setdefault(bk_, idx)
                        last_b[bk_] = idx
                    pv_idx = {k_: i_ for i_, k_ in enumerate(pv_list)}

                    def e_pv(ui):
                        kt, c, jlo = units[ui]
                        s3 = ui % 3
                        for j in range(jlo, nq):
                            a = c * 4 + j
                            idx = pv_idx[(ui, j)]
                            acc = po[a // 2][:, (a % 2) * 256:(a % 2) * 256 + 256]
                            st_, sp_ = (first_b[a // 2] == idx), (last_b[a // 2] == idx)
                            P.add("pe", lambda h, j=j, acc=acc, st_=st_, sp_=sp_: h.matmul(acc, lhsT=pt[s3][:, j * 128:(j + 1) * 128], rhs=va[:, kt, 0:256], start=st_, stop=sp_),
                                  reads=[("pt", s3), ("va", kt)], writes=[("po", a // 2)])
                            st_, sp_ = (idx == 0), (idx == len(pv_list) - 1)
                            P.add("pe", lambda h, j=j, a=a, st_=st_, sp_=sp_: h.matmul(pl[:, a:a + 1], lhsT=pt[s3][:, j * 128:(j + 1) * 128], rhs=va[:, kt, 256:257], start=st_, stop=sp_),
                                  reads=[("pt", s3), ("vaone", 0)], writes=["pl"])

                    LA = 2
                    for ui in range(min(LA, len(units))):
                        e_st(ui)
                    for ui in range(len(units)):
                        e_act(ui)
                        if ui + LA < len(units):
                            e_st(ui + LA)
                        e_pv(ui)
                    for j in range(jmin, nq):
                        t = q0 + j
                        to = t - NP
                        ys = to % 2
                        a1, a2 = j, 4 + j
                        P.add("dve", lambda h, j=j, a1=a1: h.reciprocal(out=r12[:, j:j + 1], in_=pl[:, a1:a1 + 1]), reads=["pl"], writes=[("r1", j)])
                        P.add("dve", lambda h, j=j, a2=a2: h.reciprocal(out=r12[:, 4 + j:5 + j], in_=pl[:, a2:a2 + 1]), reads=["pl"], writes=[("r2", j)])
                        P.add("dve", lambda h, j=j: h.tensor_tensor(out=r12[:, 4 + j:5 + j], in0=r12[:, 4 + j:5 + j], in1=neglam[:], op=ALU.mult), reads=[("r2", j)], writes=[("r2", j)])
                        acc1 = po[a1 // 2][:, (a1 % 2) * 256:(a1 % 2) * 256 + 256]
                        acc2 = po[a2 // 2][:, (a2 % 2) * 256:(a2 % 2) * 256 + 256]
                        P.add("act", lambda h, j=j, acc1=acc1: h.activation(out=o_sb[:], in_=acc1, func=AF.Copy, scale=r12[:, j:j + 1]), reads=[("po", a1 // 2), ("r1", j)], writes=["o_sb"])
                        P.add("dve", lambda h, j=j, acc2=acc2: h.scalar_tensor_tensor(out=o_sb[:], in0=acc2, scalar=r12[:, 4 + j:5 + j], in1=o_sb[:], op0=ALU.mult, op1=ALU.add),
                              reads=[("po", a2 // 2), ("r2", j), "o_sb"], writes=["o_sb"])
                        P.add("act", lambda h: h.activation(out=o_junk[:], in_=o_sb[:], func=AF.Square, accum_out=oss[:]), reads=["o_sb"], writes=["o_junk", "oss"])
                        P.add("act", lambda h: h.activation(out=oss[:], in_=oss[:], func=AF.Sqrt, scale=1.0 / 256, bias=epsc[:]), reads=["oss"], writes=["oss"])
                        P.add("dve", lambda h: h.reciprocal(out=oss[:], in_=oss[:]), reads=["oss"], writes=["oss"])
                        P.add("dve", lambda h, ys=ys: h.scalar_tensor_tensor(out=yb_sb[ys][:], in0=o_sb[:], scalar=oss[:], in1=sublg[:], op0=ALU.mult, op1=ALU.mult),
                              reads=["o_sb", "oss"], writes=[("yb", ys)])
                        tl = t % 4
                        P.add("dve", lambda h, ys=ys, tl=tl: h.tensor_tensor(out=yb_sb[ys][:], in0=yb_sb[ys][:], in1=gb[tl][:], op=ALU.mult),
                              reads=[("yb", ys), ("gb", tl)], writes=[("yb", ys)])
                        P.add("sp", lambda h, ys=ys, to=to: h.dma_start(out=yb_d[to * 128:(to + 1) * 128, hd * 256:(hd + 1) * 256], in_=yb_sb[ys][:]),
                              reads=[("yb", ys)], writes=[("yb_d", to)], dma=True, sem_key=("ybst", ys))

                def do_block(bi, b0, b1):
                    nt = b1 - b0
                    s = bi % 2
                    if bi + 1 < len(blocks_h):
                        load_blk(bi + 1)
                    for _ in range(pf_per_blk):
                        if pf:
                            dc_, sc_ = pf.pop(0)
                            wstep(hd + 1, dc_, sc_, False)
                    hk = [("hTb", s)]
                    for c in range(2):
                        bkr = nextbank()
                        for kc in range(16):
                            P.add("pe", lambda h, c=c, kc=kc, bkr=bkr: h.matmul(bank[bkr][:, 0:nt, :], lhsT=Wk[:, kc, c * 128:(c + 1) * 128], rhs=hTb[s][:, 0:nt, kc, :],
                                                                              start=(kc == 0), stop=(kc == 15)), reads=hk + [("W", 1)], writes=[("bk", bkr)])
                        qknorm(bkr, nt, gk[:], kT[:, c, b0 * 128:b1 * 128], [("kT", c, t) for t in range(b0, b1)], c)
                    for t in range(b0, b1):
                        tl = t - b0
                        pi = prot[0] % 4
                        prot[0] += 1
                        for kc in range(16):
                            P.add("pe", lambda h, tl=tl, kc=kc, pi=pi: h.matmul(po[pi][:, 0:256], lhsT=hTb[s][:, tl, kc, :], rhs=Wv[:, kc, :],
                                                                              start=(kc == 0), stop=(kc == 15)), reads=hk + [("W", 2)], writes=[("po", pi)])
                        if t < NP:
                            P.add("act", lambda h, t=t, pi=pi: h.activation(out=va[:, t, 0:256], in_=po[pi][:, 0:256], func=AF.Copy, scale=keepc[:]),
                                  reads=[("po", pi)], writes=[("va", t)])
                        else:
                            P.add("dve", lambda h, t=t, pi=pi: h.tensor_copy(out=va[:, t, 0:256], in_=po[pi][:, 0:256]),
                                  reads=[("po", pi)], writes=[("va", t)])
                    if b1 <= NP:
                        return
                    for c in range(2):
                        bkr = nextbank()
                        for kc in range(16):
                            P.add("pe", lambda h, c=c, kc=kc, bkr=bkr: h.matmul(bank[bkr][:, 0:nt, :], lhsT=Wq[:, kc, c * 128:(c + 1) * 128], rhs=hTb[s][:, 0:nt, kc, :],
                                                                              start=(kc == 0), stop=(kc == 15)), reads=hk + [("W", 0)], writes=[("bk", bkr)])
                        qknorm(bkr, nt, gq[:], qT[:, c, 0:nt * 128], [("qT", c)], c)
                    for t in range(max(b0, NP), b1):
                        tl = t - b0
                        pi = prot[0] % 4
                        prot[0] += 1
                        for kc in range(16):
                            P.add("pe", lambda h, tl=tl, kc=kc, pi=pi: h.matmul(po[pi][:, 0:256], lhsT=hTb[s][:, tl, kc, :], rhs=Wg[:, kc, :],
                                                                              start=(kc == 0), stop=(kc == 15)), reads=hk + [("W", 3)], writes=[("po", pi)])
                        P.add("act", lambda h, tl=tl, pi=pi: h.activation(out=gb[tl][:], in_=po[pi][:, 0:256], func=AF.Sigmoid),
                              reads=[("po", pi)], writes=[("gb", tl)])
                    jmin = max(0, NP - b0)
                    if wide:
                        attention(b0, b1, jmin, 0)
                    else:
                        for t in range(max(b0, NP), b1):
                            attention(t, t + 1, 0, (t - b0) * 128)

                load_blk(0)
                for bi, (b0, b1) in enumerate(blocks_h):
                    do_block(bi, b0, b1)
                assert not pf
                P.emit()

        dstack.close()
        for hm in range(4):
            with contextlib.ExitStack() as es:
                A = Alloc(nc, es, f"m{hm}")
                P = Prog(ctx)
                Wq = A.sb("Wq", [128, 16, 256], BF16)
                Wk = A.sb("Wk", [128, 16, 256], BF16)
                Wv = A.sb("Wv", [128, 16, 512], BF16)
                Wo = A.sb("Wo", [128, 16, 512], BF16)
                Wga = A.sb("Wga", [128, 16, 512], BF16)
                hTb = [A.sb(f"hTb{i}", [128, 4, 16, 128], BF16) for i in range(2)]
                cw = A.sb("cw", [128, 4, 4], F32)
                cb = A.sb("cb", [128, 4], F32)
                mgbc = A.sb("mgbc", [128, 512], F32)
                raw = [A.sb(f"raw{i}", [128, 3 + 512], F32) for i in range(4)]
                acc2 = [A.sb(f"acc{i}", [128, 512], F32) for i in range(2)]
                qT = A.sb("qT", [128, 2, 512], BF16)
                kT = A.sb("kT", [128, 2, 512], BF16)
                vt = A.sb("vt", [128, 4, 512], BF16)
                gate = A.sb("gate", [128, 4, 512], F32)
                gtmp = [A.sb(f"gtmp{i}", [128, 512], F32) for i in range(2)]
                kp = A.sb("kp", [128, 4, 256], BF16)
                pTs = A.sb("pTs", [128, 4, 128], BF16)
                Cst = A.sb("Cst", [128, 2, 512], F32)
                Cb = A.sb("Cb", [128, 2, 512], BF16)
                nst = A.sb("nst", [128, 2], F32)
                nb = A.sb("nb", [128, 2], BF16)
                dd = A.sb("dd", [128, 1], F32)
                hmS = A.sb("hmS", [128, 512], F32)
                hjunk = A.sb("hjunk", [128, 512], F32)
                hss = A.sb("hss", [128, 1], F32)
                ybl = A.sb("ybl", [128, 512], F32)
                ysb4 = [A.sb(f"ysb{i}", [128, 512], BF16) for i in range(4)]
                yTs = A.sb("yTs", [128, 4, 128], BF16)
                pwu = A.sb("pwu", [128, 16, 256], BF16)
                pwd = A.sb("pwd", [128, D], BF16)
                pq = [A.ps(f"pq{i}", [128, 4, 128]) for i in range(2)]
                pv = A.ps("pv")
                psd = A.ps("psd")
                pnum = A.ps("pnum")
                pdc = [A.ps(f"pdc{i}") for i in range(2)]
                vbanks = [(pv, "pv"), (pnum, "pnum"), (pdc[0], ("pdc", 0)), (pdc[1], ("pdc", 1))]
                vrot = [0]
                ptr = A.ps("ptr", [128, 1024], BF16)
                wstg = [A.sb(f"wstg{i}", [128, 16, 256], F32) for i in range(2)]
                wl = [(Wk, "Wk", 0, 1024 + hm * 256), (Wv, "Wv", 0, OFF_AV + hm * 512), (Wv, "Wv", 256, OFF_AV + hm * 512 + 256), (Wq, "Wq", 0, hm * 256),
                      (Wo, "Wo", 0, OFF_AO + hm * 512), (Wo, "Wo", 256, OFF_AO + hm * 512 + 256), (Wga, "Wga", 0, OFF_GA + hm * 512), (Wga, "Wga", 256, OFF_GA + hm * 512 + 256)]
                for i, (wt, wkey, dc, sc) in enumerate(wl):
                    P.add("sp", lambda h, i=i, sc=sc: h.dma_start(out=wstg[i % 2][:], in_=wcols(w_in, sc, 256)), writes=[("wstg", i % 2)], dma=True)
                    P.add("pool", lambda h, i=i, wt=wt, dc=dc: h.tensor_copy(out=wt[:, :, dc:dc + 256], in_=wstg[i % 2][:]), reads=[("wstg", i % 2)], writes=[wkey])
                for j in range(4):
                    ch0 = (0 if j < 2 else 1024) + hm * 256 + (j % 2) * 128
                    P.add("sp", lambda h, j=j, ch0=ch0: h.dma_start(out=cw[:, j, :], in_=qk_conv_wT[ch0:ch0 + 128, :]), writes=[("cw", j)], dma=True, sem_key="cwl")
                    P.add("sp", lambda h, j=j, ch0=ch0: h.dma_start(out=cb[:, j:j + 1], in_=qk_conv_b[ch0:ch0 + 128].rearrange("(p o) -> p o", o=1)),
                          writes=[("cb", j)], dma=True, sem_key="cwl")
                P.add("sp", lambda h, hm=hm: h.dma_start(out=mgbc[:], in_=mlstm_norm_g[hm * 512:(hm + 1) * 512].partition_broadcast(128)), writes=["mgbc"], dma=True)
                P.add("dve", lambda h: h.memset(Cst[:], 0.0), writes=["Cst"])
                P.add("dve", lambda h: h.memset(nst[:], 0.0), writes=["nst"])
                P.add("dve", lambda h: h.memset(Cb[:], 0.0), writes=["Cb"])
                P.add("dve", lambda h: h.memset(nb[:], 0.0), writes=["nb"])
                for j in range(4):
                    P.add("dve", lambda h, j=j: h.memset(raw[j][:, 0:3], 0.0), writes=[("raw", j)])

                def load_blk(bi):
                    b0, b1 = blocks[bi]
                    s = bi % 2
                    P.add("sp",
                          lambda h, b0=b0, b1=b1, s=s: h.dma_start(out=hTb[s][:, 0:b1 - b0], in_=hT_d[b0:b1].rearrange("t p c n -> p t c n")),
                          writes=[("hTb", s)], dma=True)

                def do_mblock(bi, b0, b1):
                    nt = b1 - b0
                    n = nt * 128
                    s = bi % 2
                    if bi + 1 < len(blocks):
                        load_blk(bi + 1)
                    hk = [("hTb", s)]
                    own_blk = b1 > NP
                    js = [j for j in range(4) if not (j < 2 and not own_blk)]

                    def conv_head(j):
                        W = Wq if j < 2 else Wk
                        c = j % 2
                        pj = pq[j % 2]
                        ac = acc2[j % 2]
                        for kc in range(16):
                            P.add("pe", lambda h, kc=kc: h.matmul(pj[:, 0:nt, :], lhsT=W[:, kc, c * 128:(c + 1) * 128], rhs=hTb[s][:, 0:nt, kc, :],
                                                                  start=(kc == 0), stop=(kc == 15)), reads=hk + ["Wq", "Wk"], writes=[("pq", j % 2)])
                        P.add("act", lambda h: h.activation(out=raw[j][:, 3:3 + n].rearrange("p (a b) -> p a b", b=128), in_=pj[:, 0:nt, :], func=AF.Copy),
                              reads=[("pq", j % 2)], writes=[("raw", j)])
                        P.add("act", lambda h: h.activation(out=ac[:, 0:n], in_=raw[j][:, 3:3 + n], func=AF.Identity, scale=cw[:, j, 3:4], bias=cb[:, j:j + 1]),
                              reads=[("raw", j), ("cw", j), ("cb", j)], writes=[("acc", j % 2)])
                        for tap in range(3):
                            P.add("dve", lambda h, tap=tap: h.scalar_tensor_tensor(out=ac[:, 0:n], in0=raw[j][:, tap:tap + n], scalar=cw[:, j, tap:tap + 1], in1=ac[:, 0:n],
                                                                               op0=ALU.mult, op1=ALU.add), reads=[("raw", j), ("acc", j % 2)], writes=[("acc", j % 2)])

                    def conv_tail(j):
                        c = j % 2
                        ac = acc2[j % 2]
                        dst = qT if j < 2 else kT
                        P.add("act", lambda h: h.activation(out=dst[:, c, 0:n], in_=ac[:, 0:n], func=AF.Silu), reads=[("acc", j % 2)],
                              writes=[("qT" if j < 2 else "kT", c)])
                        P.add("pool", lambda h: h.tensor_copy(out=raw[j][:, 0:3], in_=raw[j][:, n:n + 3]), reads=[("raw", j)], writes=[("raw", j)])

                    for i, j in enumerate(js):
                        conv_head(j)
                        if i >= 1:
                            conv_tail(js[i - 1])
                    conv_tail(js[-1])
                    def proj_tok(tl, W, wkey):
                        bkt, bkey = vbanks[vrot[0] % 4]
                        vrot[0] += 1
                        for kc in range(16):
                            P.add("pe", lambda h, kc=kc: h.matmul(bkt[:, 0:512], lhsT=hTb[s][:, tl, kc, :], rhs=W[:, kc, :], start=(kc == 0), stop=(kc == 15)),
                                  reads=hk + [wkey], writes=[bkey])
                        return bkt, bkey

                    for t in range(b0, b1):
                        tl = t - b0
                        bkt, bkey = proj_tok(tl, Wv, "Wv")
                        if t < NP:
                            P.add("act", lambda h, tl=tl, bkt=bkt: h.activation(out=vt[:, tl, :], in_=bkt[:, 0:512], func=AF.Copy, scale=keepc[:]), reads=[bkey], writes=[("vt", tl)])
                        else:
                            P.add("dve", lambda h, tl=tl, bkt=bkt: h.tensor_copy(out=vt[:, tl, :], in_=bkt[:, 0:512]), reads=[bkey], writes=[("vt", tl)])
                        if t >= NP:
                            bkt, bkey = proj_tok(tl, Wo, "Wo")
                            P.add("act", lambda h, tl=tl, bkt=bkt: h.activation(out=gate[:, tl, :], in_=bkt[:, 0:512], func=AF.Sigmoid), reads=[bkey], writes=[("gate", tl)])
                            bkt, bkey = proj_tok(tl, Wga, "Wga")
                            gs = tl % 2
                            P.add("act", lambda h, bkt=bkt, gs=gs: h.activation(out=gtmp[gs][:], in_=bkt[:, 0:512], func=AF.Sigmoid), reads=[bkey], writes=[("gtmp", gs)])
                            P.add("pool", lambda h, tl=tl, gs=gs: h.tensor_tensor(out=gate[:, tl, :], in0=gate[:, tl, :], in1=gtmp[gs][:], op=ALU.mult),
                                  reads=[("gate", tl), ("gtmp", gs)], writes=[("gate", tl)])
                    for half in range(0, nt, 2):
                        tls = list(range(half, min(half + 2, nt)))
                        for tl in tls:
                            for c in range(2):
                                o0 = (tl - half) * 256 + c * 128
                                P.add("pe", lambda h, c=c, tl=tl, o0=o0: h.transpose(out=ptr[:, o0:o0 + 128], in_=kT[:, c, tl * 128:(tl + 1) * 128], identity=ident_b[:]),
                                      reads=[("kT", c)], writes=["ptr"])
                        for tl in tls:
                            wcol = WT[:, hm, b0 + tl:b0 + tl + 1]
                            o0 = (tl - half) * 256
                            P.add("act", lambda h, tl=tl, o0=o0, wcol=wcol: h.activation(out=kp[:, tl, :], in_=ptr[:, o0:o0 + 256], func=AF.Copy, scale=wcol),
                                  reads=["ptr"], writes=[("kp", tl)])
                    own_tls = [tl for tl in range(nt) if b0 + tl >= NP]
                    for tl in own_tls:
                        for c in range(2):
                            P.add("pe", lambda h, c=c, tl=tl: h.matmul(psd[:, tl * 128:(tl + 1) * 128], lhsT=kT[:, c, tl * 128:(tl + 1) * 128], rhs=qT[:, c, tl * 128:(tl + 1) * 128],
                                                                   start=(c == 0), stop=(c == 1)), reads=[("kT", c), ("qT", c)], writes=["psd"])
                    for tl in own_tls:
                        wcol = WT[:, hm, b0 + tl:b0 + tl + 1]
                        P.add("dve", lambda h, tl=tl, wcol=wcol: h.scalar_tensor_tensor(out=pTs[:, tl, :], in0=psd[:, tl * 128:(tl + 1) * 128], scalar=wcol, in1=maskU[:],
                                                                                    op0=ALU.mult, op1=ALU.mult), reads=["psd"], writes=[("pTs", tl)])
                    for t in range(b0, b1):
                        tl = t - b0
                        to = t - NP
                        own = t >= NP
                        dcol = DEL[:, hm, t:t + 1]
                        kcol = onecb if own else keepcb
                        for c in range(2):
                            P.add("pe", lambda h, c=c, tl=tl: h.matmul(pdc[c][:, 0:512], lhsT=kp[:, tl, c * 128:(c + 1) * 128], rhs=vt[:, tl, :], start=True, stop=True),
                                  reads=[("kp", tl), ("vt", tl)], writes=[("pdc", c)])
                        for c in range(2):
                            P.add("pe", lambda h, c=c, tl=tl, kcol=kcol: h.matmul(pv[:, 8 + c:9 + c], lhsT=kp[:, tl, c * 128:(c + 1) * 128], rhs=kcol[:], start=True, stop=True),
                                  reads=[("kp", tl)], writes=["pv"])
                        if own:
                            P.add("pe", lambda h, tl=tl: h.matmul(pnum[:, 0:512], lhsT=pTs[:, tl, :], rhs=vt[:, tl, :], start=True, stop=False), reads=[("pTs", tl), ("vt", tl)], writes=["pnum"])
                            for c in range(2):
                                P.add("pe", lambda h, c=c, tl=tl: h.matmul(pnum[:, 0:512], lhsT=qT[:, c, tl * 128:(tl + 1) * 128], rhs=Cb[:, c, :], start=False, stop=(c == 1)),
                                      reads=[("qT", c), "Cb"], writes=["pnum"])
                            P.add("pe", lambda h, tl=tl: h.matmul(pv[:, 0:1], lhsT=pTs[:, tl, :], rhs=onecb[:], start=True, stop=False), reads=[("pTs", tl)], writes=["pv"])
                            for c in range(2):
                                P.add("pe", lambda h, c=c, tl=tl: h.matmul(pv[:, 0:1], lhsT=qT[:, c, tl * 128:(tl + 1) * 128], rhs=nb[:, c:c + 1], start=False, stop=(c == 1)),
                                      reads=[("qT", c), "nb"], writes=["pv"])
                        for c in range(2):
                            P.add("dve", lambda h, c=c, dcol=dcol: h.scalar_tensor_tensor(out=Cst[:, c, :], in0=Cst[:, c, :], scalar=dcol, in1=pdc[c][:, 0:512], op0=ALU.mult, op1=ALU.add),
                                  reads=["Cst", ("pdc", c)], writes=["Cst"])
                        P.add("dve", lambda h, dcol=dcol: h.scalar_tensor_tensor(out=nst[:], in0=nst[:], scalar=dcol, in1=pv[:, 8:10], op0=ALU.mult, op1=ALU.add),
                              reads=["nst", "pv"], writes=["nst"])
                        if own:
                            P.add("act", lambda h: h.activation(out=dd[:], in_=pv[:, 0:1], func=AF.Abs), reads=["pv"], writes=["dd"])
                        if t + 1 < NT and t + 1 >= NP:
                            ncol = DEL[:, hm, t + 1:t + 2]
                            P.add("act", lambda h, ncol=ncol: h.activation(out=Cb[:].rearrange("p a b -> p (a b)"), in_=Cst[:].rearrange("p a b -> p (a b)"), func=AF.Copy, scale=ncol),
                                  reads=["Cst"], writes=["Cb"])
                            P.add("act", lambda h, ncol=ncol: h.activation(out=nb[:], in_=nst[:], func=AF.Copy, scale=ncol), reads=["nst"], writes=["nb"])
                        if own:
                            ccol = CL[:, hm, t:t + 1]
                            P.add("dve", lambda h, ccol=ccol: h.tensor_tensor(out=dd[:], in0=dd[:], in1=ccol, op=ALU.max), reads=["dd"], writes=["dd"])
                            P.add("dve", lambda h: h.reciprocal(out=dd[:], in_=dd[:]), reads=["dd"], writes=["dd"])
                            P.add("act", lambda h: h.activation(out=hmS[:], in_=pnum[:, 0:512], func=AF.Copy, scale=dd[:]), reads=["pnum", "dd"], writes=["hmS"])
                            P.add("act", lambda h: h.activation(out=hjunk[:], in_=hmS[:], func=AF.Square, accum_out=hss[:]), reads=["hmS"], writes=["hjunk", "hss"])
                            P.add("act", lambda h: h.activation(out=hss[:], in_=hss[:], func=AF.Sqrt, scale=1.0 / 512, bias=epsc[:]), reads=["hss"], writes=["hss"])
                            P.add("dve", lambda h: h.reciprocal(out=hss[:], in_=hss[:]), reads=["hss"], writes=["hss"])
                            P.add("sp", lambda h, to=to, hm=hm: h.dma_start(out=ybl[:], in_=yb_d[to * 128:(to + 1) * 128, hm * 512:(hm + 1) * 512]), writes=["ybl"], dma=True)
                            P.add("dve", lambda h: h.scalar_tensor_tensor(out=hmS[:], in0=hmS[:], scalar=hss[:], in1=mgbc[:], op0=ALU.mult, op1=ALU.mult),
                                  reads=["hmS", "hss", "mgbc"], writes=["hmS"])
                            P.add("pool", lambda h, tl=tl: h.tensor_tensor(out=hmS[:], in0=hmS[:], in1=gate[:, tl, :], op=ALU.mult), reads=["hmS", ("gate", tl)], writes=["hmS"])
                            P.add("pool", lambda h, tl=tl: h.tensor_tensor(out=ysb4[tl][:], in0=hmS[:], in1=ybl[:], op=ALU.add), reads=["hmS", "ybl"], writes=[("ysb", tl)])
                            if debug:
                                P.add("dve", lambda h: h.tensor_tensor(out=hjunk[:], in0=hmS[:], in1=ybl[:], op=ALU.add), reads=["hmS", "ybl", "hjunk"], writes=["hjunk"])
                                P.add("sp", lambda h, to=to, hm=hm: h.dma_start(out=dbg["y"][to * 128:(to + 1) * 128, hm * 512:(hm + 1) * 512], in_=hjunk[:]),
                                      reads=["hjunk"], writes=[("dbgy", to)], dma=True, sem_key="dbgy")
                    for t in range(max(b0, NP), b1):
                        tl = t - b0
                        to = t - NP
                        for c in range(4):
                            P.add("pe", lambda h, c=c, tl=tl: h.transpose(out=ptr[:, 512 + c * 128:512 + (c + 1) * 128], in_=ysb4[tl][:, c * 128:(c + 1) * 128], identity=ident_b[:]),
                                  reads=[("ysb", tl)], writes=["ptr"])
                        P.add("act", lambda h: h.activation(out=yTs[:].rearrange("p a b -> p (a b)"), in_=ptr[:, 512:1024], func=AF.Copy), reads=["ptr"], writes=["yTs"])
                        P.add("sp", lambda h, to=to, hm=hm: h.dma_start(out=yT_d[to, :, hm * 4:(hm + 1) * 4, :], in_=yTs[:]), reads=["yTs"], writes=[("yT_d", to)], dma=True,
                              sem_key="yTst")
                pw_j = list(range(hm * (NFF // 4), (hm + 1) * (NFF // 4)))
                pw_per_blk = -(-len(pw_j) // len(blocks))
                pw_state = {"next": 0, "pending": None}

                def pw_step():
                    if pw_state["pending"] is not None:
                        j = pw_state["pending"]
                        P.add("sp", lambda h, j=j: h.dma_start(out=wup_d[j], in_=pwu[:]), reads=["pwu"], writes=[("wup_d", j)], dma=True, sem_key="pwus")
                        P.add("sp", lambda h, j=j: h.dma_start(out=wdn_d[j], in_=pwd[:]), reads=["pwd"], writes=[("wdn_d", j)], dma=True, sem_key="pwds")
                        pw_state["pending"] = None
                    if pw_state["next"] < len(pw_j):
                        j = pw_j[pw_state["next"]]
                        pw_state["next"] += 1
                        pwdf = wstg[1][:].rearrange("p a b -> p (a b)")[:, 0:D]
                        P.add("sp", lambda h, j=j: h.dma_start(out=wstg[0][:, :, 0:128], in_=wcols(w_up, j * 128, 128)), writes=[("wstg", 0)], dma=True, sem_key="pwl0")
                        P.add("sp", lambda h, j=j: h.dma_start(out=wstg[0][:, :, 128:256], in_=wcols(w_up, DFF + j * 128, 128)), writes=[("wstg", 0)], dma=True, sem_key="pwl1")
                        P.add("sp", lambda h, j=j, pwdf=pwdf: h.dma_start(out=pwdf, in_=w_down[j * 128:(j + 1) * 128, :]), writes=[("wstg", 1)], dma=True, sem_key="pwl2")
                        P.add("pool", lambda h: h.tensor_copy(out=pwu[:], in_=wstg[0][:]), reads=[("wstg", 0)], writes=["pwu"])
                        P.add("pool", lambda h, pwdf=pwdf: h.tensor_copy(out=pwd[:], in_=pwdf), reads=[("wstg", 1)], writes=["pwd"])
                        pw_state["pending"] = j

                load_blk(0)
                for bi, (b0_, b1_) in enumerate(blocks):
                    for _ in range(pw_per_blk):
                        pw_step()
                    do_mblock(bi, b0_, b1_)
                pw_step()
                assert pw_state["next"] == len(pw_j) and pw_state["pending"] is None
                P.emit()

        with contextlib.ExitStack() as es:
            A = Alloc(nc, es, "p2a")
            P = Prog(ctx)
            Wout = A.sb("Wout", [128, 16, D], BF16)
            g2bc = A.sb("g2bc", [128, D], F32)
            yTt = [A.sb(f"yTt{i}", [128, 16, 128], BF16) for i in range(2)]
            xt = [A.sb(f"xt{i}", [128, D], F32) for i in range(2)]
            xm = [A.sb(f"xm{i}", [128, D], F32) for i in range(2)]
            junk = A.sb("junk", [128, D], BF16)
            hn2 = [A.sb(f"hn{i}", [128, D], BF16) for i in range(2)]
            ss2 = [A.sb(f"ss{i}", [128, 1], F32) for i in range(2)]
            h2T = [A.sb(f"h2T{i}", [128, 16, 128], BF16) for i in range(2)]
            po = [A.ps(f"po{i}") for i in range(4)]
            tp = [A.ps(f"tp{i}", [128, 8, 128], BF16) for i in range(2)]
            wstg = [A.sb(f"wstg{i}", [128, 16, 256], F32) for i in range(2)]
            for i in range(8):
                P.add("sp", lambda h, i=i: h.dma_start(out=wstg[i % 2][:], in_=wcols(w_out, i * 256, 256)), writes=[("wstg", i % 2)], dma=True)
                P.add("pool", lambda h, i=i: h.tensor_copy(out=Wout[:, :, i * 256:(i + 1) * 256], in_=wstg[i % 2][:]), reads=[("wstg", i % 2)], writes=[("Wout", i // 2)])
            P.add("sp", lambda h: h.dma_start(out=g2bc[:], in_=norm2_g.partition_broadcast(128)), writes=["g2bc"], dma=True)
            def loads_2a(to):
                P.add("sp", lambda h: h.dma_start(out=yTt[to % 2][:], in_=yT_d[to]), writes=[("yTt", to % 2)], dma=True)
                P.add("sp", lambda h: h.dma_start(out=xt[to % 2][:], in_=xin[(to + NP) * 128:(to + NP + 1) * 128, :]), writes=[("xt", to % 2)], dma=True)

            def stage_a(to):
                t = to + NP
                s = to % 2
                if to == 0:
                    loads_2a(0)
                if to + 1 < NO:
                    loads_2a(to + 1)
                for q4 in range(4):
                    for kc in range(16):
                        P.add("pe", lambda h, q4=q4, kc=kc: h.matmul(po[q4][:, 0:512], lhsT=yTt[s][:, kc, :], rhs=Wout[:, kc, q4 * 512:(q4 + 1) * 512], start=(kc == 0), stop=(kc == 15)),
                              reads=[("yTt", s), ("Wout", q4)], writes=[("po", q4)])
                    P.add("dve", lambda h, q4=q4: h.tensor_tensor(out=xm[s][:, q4 * 512:(q4 + 1) * 512], in0=xt[s][:, q4 * 512:(q4 + 1) * 512], in1=po[q4][:, 0:512], op=ALU.add),
                          reads=[("xt", s), ("po", q4)], writes=[("xm", s, q4)])
                xk = [("xm", s, q4) for q4 in range(4)]
                P.add("sp", lambda h: h.dma_start(out=xm_d[to * 128:(to + 1) * 128, :], in_=xm[s][:]), reads=xk, writes=[("xm_d", to)], dma=True, sem_key=("xmst", s))
                if debug:
                    P.add("sp", lambda h: h.dma_start(out=dbg["xm"][to * 128:(to + 1) * 128, :], in_=xm[s][:]), reads=xk, writes=[("dbgxm", to)], dma=True, sem_key=("dbgxm", s))
                P.add("act", lambda h: h.activation(out=junk[:], in_=xm[s][:], func=AF.Square, accum_out=ss2[s][:]), reads=xk, writes=["junk", ("ss", s)])
                P.add("act", lambda h: h.activation(out=ss2[s][:], in_=ss2[s][:], func=AF.Sqrt, scale=1.0 / D, bias=epsc[:]), reads=[("ss", s)], writes=[("ss", s)])
                P.add("dve", lambda h: h.reciprocal(out=ss2[s][:], in_=ss2[s][:]), reads=[("ss", s)], writes=[("ss", s)])
                P.add("dve", lambda h: h.scalar_tensor_tensor(out=hn2[s][:], in0=xm[s][:], scalar=ss2[s][:], in1=g2bc[:], op0=ALU.mult, op1=ALU.mult),
                      reads=xk + [("ss", s), "g2bc"], writes=[("hn", s)])

            def stage_b(to):
                s = to % 2
                for half in range(2):
                    for c in range(8):
                        cc = half * 8 + c
                        P.add("pe", lambda h, half=half, c=c, cc=cc: h.transpose(out=tp[half][:, c, :], in_=hn2[s][:, cc * 128:(cc + 1) * 128], identity=ident_b[:]),
                              reads=[("hn", s)], writes=[("tp", half)])
                    if half == 0:
                        P.add("act", lambda h: h.activation(out=h2T[s][:, 0:8, :], in_=tp[0][:], func=AF.Copy), reads=[("tp", 0)], writes=[("h2T", s, 0)])
                    else:
                        P.add("dve", lambda h: h.tensor_copy(out=h2T[s][:, 8:16, :], in_=tp[1][:]), reads=[("tp", 1)], writes=[("h2T", s, 1)])
                P.add("sp", lambda h: h.dma_start(out=h2T_d[to], in_=h2T[s][:]), reads=[("h2T", s, 0), ("h2T", s, 1)], writes=[("h2T_d", to)], dma=True, sem_key=("h2st", s))

            stage_a(0)
            for to in range(NO):
                if to + 1 < NO:
                    stage_a(to + 1)
                stage_b(to)
            P.emit()

        oblocks = [(b0, min(b0 + 4, NO)) for b0 in range(0, NO, 4)]
        GRP = 4
        with contextlib.ExitStack() as es:
            A = Alloc(nc, es, "p2b")
            P = Prog(ctx)
            h2b = [A.sb(f"h2b{i}", [128, 4, 16, 128], BF16) for i in range(2)]
            wus = [A.sb(f"wus{i}", [128, 16, 256], BF16) for i in range(3)]
            wds = [A.sb(f"wds{i}", [128, GRP, D], BF16) for i in range(2)]
            fcw = A.sb("fcw", [128, 2 * NFF, 3], F32)
            fcb = A.sb("fcb", [128, 2 * NFF], F32)
            halo = A.sb("halo", [128, 2 * NFF, 2], F32)
            rawg = [A.sb(f"rawg{i}", [128, 2 + 512], F32) for i in range(2)]
            accg = [A.sb(f"accg{i}", [128, 512], F32) for i in range(2)]
            sg = A.sb("sg", [128, 512], F32)
            aT = [A.sb(f"aT{i}", [128, GRP, 512], BF16) for i in range(2)]
            oacc = A.sb("oacc", [128, 4, D], F32)
            xmt = [A.sb(f"xmt{i}", [128, D], F32) for i in range(2)]
            pu = [A.ps(f"pu{i}", [128, 4, 128]) for i in range(4)]
            pd = [A.ps(f"pd{i}") for i in range(4)]
            P.add("sp", lambda h: h.dma_start(out=fcw[:], in_=ffn_conv_wP[:, :, :]), writes=["fcw"], dma=True)
            P.add("sp", lambda h: h.dma_start(out=fcb[:], in_=ffn_conv_bP[:, :]), writes=["fcb"], dma=True)
            P.add("dve", lambda h: h.memset(halo[:], 0.0), writes=["halo"])
            ngrp = NFF // GRP
            ucount = 0
            dcount = 0
            def h2b_load(bi):
                b0, b1 = oblocks[bi]
                P.add("sp", lambda h: h.dma_start(out=h2b[bi % 2][:, 0:b1 - b0], in_=h2T_d[b0:b1].rearrange("t p c n -> p t c n")), writes=[("h2b", bi % 2)], dma=True)

            h2b_load(0)
            for bi, (b0, b1) in enumerate(oblocks):
                nt = b1 - b0
                n = nt * 128
                s = bi % 2
                if bi + 1 < len(oblocks):
                    h2b_load(bi + 1)
                for gi in range(ngrp):
                    ga = gi % 2
                    dsl = dcount % 2
                    dcount += 1
                    P.add("sp", lambda h, gi=gi, dsl=dsl: h.dma_start(out=wds[dsl][:], in_=wdn_d[gi * GRP:(gi + 1) * GRP].rearrange("j p n -> p j n")), writes=[("wds", dsl)], dma=True)
                    for jj in range(GRP):
                        j = gi * GRP + jj
                        us = ucount % 3
                        ucount += 1
                        P.add("sp", lambda h, j=j, us=us: h.dma_start(out=wus[us][:], in_=wup_d[j]), writes=[("wus", us)], dma=True)
                        for gv in range(2):
                            ch = gv * NFF + j
                            pj = pu[(2 * jj + gv) % 4]
                            pkey = ("pu", (2 * jj + gv) % 4)
                            for kc in range(16):
                                P.add("pe", lambda h, s=s, gv=gv, kc=kc, pj=pj, us=us, nt=nt: h.matmul(pj[:, 0:nt, :], lhsT=wus[us][:, kc, gv * 128:(gv + 1) * 128], rhs=h2b[s][:, 0:nt, kc, :],
                                                                                                 start=(kc == 0), stop=(kc == 15)), reads=[("h2b", s), ("wus", us)], writes=[pkey])
                            rg = rawg[gv]
                            P.add("pool", lambda h, rg=rg, ch=ch: h.tensor_copy(out=rg[:, 0:2], in_=halo[:, ch, :]), reads=["halo"], writes=[("rawg", gv)])
                            P.add("act", lambda h, rg=rg, pj=pj, nt=nt, n=n: h.activation(out=rg[:, 2:2 + n].rearrange("p (a b) -> p a b", b=128), in_=pj[:, 0:nt, :], func=AF.Copy),
                                  reads=[pkey], writes=[("rawg", gv)])
                            P.add("act", lambda h, rg=rg, gv=gv, ch=ch, n=n: h.activation(out=accg[gv][:, 0:n], in_=rg[:, 2:2 + n], func=AF.Identity, scale=fcw[:, ch, 2:3], bias=fcb[:, ch:ch + 1]),
                                  reads=[("rawg", gv), "fcw", "fcb"], writes=[("accg", gv)])
                            for tap in range(2):
                                P.add("dve", lambda h, rg=rg, gv=gv, ch=ch, n=n, tap=tap: h.scalar_tensor_tensor(out=accg[gv][:, 0:n], in0=rg[:, tap:tap + n], scalar=fcw[:, ch, tap:tap + 1],
                                                                                                       in1=accg[gv][:, 0:n], op0=ALU.mult, op1=ALU.add),
                                      reads=[("rawg", gv), ("accg", gv)], writes=[("accg", gv)])
                            P.add("pool", lambda h, rg=rg, ch=ch, n=n: h.tensor_copy(out=halo[:, ch, :], in_=rg[:, n:n + 2]), reads=[("rawg", gv)], writes=["halo"])
                        P.add("act", lambda h, n=n: h.activation(out=sg[:, 0:n], in_=accg[0][:, 0:n], func=AF.Silu), reads=[("accg", 0)], writes=["sg"])
                        P.add("dve", lambda h, ga=ga, jj=jj, n=n: h.tensor_tensor(out=aT[ga][:, jj, 0:n], in0=sg[:, 0:n], in1=accg[1][:, 0:n], op=ALU.mult),
                              reads=["sg", ("accg", 1)], writes=[("aT", ga, jj)])
                    for tl in range(nt):
                        for q4 in range(4):
                            pdi = (tl * 4 + q4) % 4
                            for jj in range(GRP):
                                P.add("pe", lambda h, ga=ga, jj=jj, tl=tl, q4=q4, pdi=pdi, dsl=dsl: h.matmul(pd[pdi][:, 0:512], lhsT=aT[ga][:, jj, tl * 128:(tl + 1) * 128],
                                                                                                     rhs=wds[dsl][:, jj, q4 * 512:(q4 + 1) * 512], start=(jj == 0), stop=(jj == GRP - 1)),
                                      reads=[("aT", ga, jj), ("wds", dsl)], writes=[("pd", pdi)])
                            if gi == 0:
                                P.add("dve", lambda h, tl=tl, q4=q4, pdi=pdi: h.tensor_copy(out=oacc[:, tl, q4 * 512:(q4 + 1) * 512], in_=pd[pdi][:, 0:512]),
                                      reads=[("pd", pdi)], writes=[("oacc", tl, q4)])
                            else:
                                P.add("dve", lambda h, tl=tl, q4=q4, pdi=pdi: h.tensor_tensor(out=oacc[:, tl, q4 * 512:(q4 + 1) * 512], in0=oacc[:, tl, q4 * 512:(q4 + 1) * 512],
                                                                                             in1=pd[pdi][:, 0:512], op=ALU.add), reads=[("pd", pdi), ("oacc", tl, q4)], writes=[("oacc", tl, q4)])
                for tl in range(nt):
                    to = b0 + tl
                    xs = tl % 2
                    ok = [("oacc", tl, q4) for q4 in range(4)]
                    P.add("sp", lambda h, to=to, xs=xs: h.dma_start(out=xmt[xs][:], in_=xm_d[to * 128:(to + 1) * 128, :]), writes=[("xmt", xs)], dma=True)
                    P.add("pool", lambda h, tl=tl, xs=xs: h.tensor_tensor(out=oacc[:, tl, :], in0=oacc[:, tl, :], in1=xmt[xs][:], op=ALU.add), reads=ok + [("xmt", xs)], writes=ok)
                    P.add("sp", lambda h, tl=tl, to=to: h.dma_start(out=out[to * 128:(to + 1) * 128, :], in_=oacc[:, tl, :]),
                          reads=ok, writes=[("out", to)], dma=True, sem_key=("ost", tl % 2))
            P.emit()
        print(f"[kernel] ops={ctx.n_ops} waits={ctx.n_waits}")
    return nc


_ALIBI_CACHE = {}


def _consts(NT):
    slopes = 2.0 ** (-8.0 * np.arange(1, 9) / 8)
    p = np.arange(128)[:, None]
    d = np.arange(NT)[None, :]
    tab = np.concatenate([s * (p - 64.0 - 128.0 * d) for s in slopes], axis=1).astype(np.float32)
    e = np.arange(NT + 4)[None, :]
    tabW = np.concatenate([s * (p + 128.0 - 128.0 * e) for s in slopes], axis=1).astype(np.float32)
    return {"c_ident": np.eye(128, dtype=np.float32), "c_maskU": np.triu(np.ones((128, 128), np.float32)), "c_alibi": tab, "c_alibiW": tabW}


def _param_maps(inputs):
    m = {}
    for k in ("norm1_g", "w_in", "if_bias", "qk_conv_b", "mlstm_norm_g", "q_norm_g", "k_norm_g", "subln_g", "w_out", "norm2_g", "w_up", "ffn_conv_b", "w_down"):
        m[k] = np.ascontiguousarray(np.asarray(inputs[k], dtype=np.float32)[0])
    m["diff_lambda"] = np.ascontiguousarray(np.asarray(inputs["diff_lambda"], dtype=np.float32)[0].reshape(512))
    m["qk_conv_wT"] = np.ascontiguousarray(np.asarray(inputs["qk_conv_w"], dtype=np.float32)[0].T)
    fw = np.asarray(inputs["ffn_conv_w"], dtype=np.float32)[0]
    m["ffn_conv_wP"] = np.ascontiguousarray(fw.T.reshape(2 * NFF, 128, 3).transpose(1, 0, 2))
    m["ffn_conv_bP"] = np.ascontiguousarray(m.pop("ffn_conv_b").reshape(2 * NFF, 128).T)
    return m


def run_cores(nc, pm, xins, keeps, NT):
    cm = _consts(NT)
    in_maps = []
    for xi, kp in zip(xins, keeps):
        mm = dict(pm)
        mm.update(cm)
        mm["xin"] = np.ascontiguousarray(xi, dtype=np.float32)
        mm["c_keep"] = np.full((128, 1), kp, np.float32)
        in_maps.append(mm)
    res = run_bass_kernel_spmd(nc, in_maps, core_ids=list(range(len(in_maps))))
    return res.results


def kernel(**inputs):
    NT, NP = 64, 31
    x = np.asarray(inputs["x"], dtype=np.float32)
    B, S, _ = x.shape
    assert B == 4 and S == 8192
    nc = build(NT, NP)
    pm = _param_maps(inputs)
    xins, keeps = [], []
    for b in range(4):
        x0 = np.zeros((NT * 128, D), np.float32)
        x0[NP * 128:] = x[b, 0:(NT - NP) * 128]
        xins.append(x0)
        keeps.append(0.0)
        xins.append(x[b])
        keeps.append(1.0)
    res = run_cores(nc, pm, xins, keeps, NT)
    out = np.empty((B, S, D), np.float32)
    for b in range(4):
        o0 = np.asarray(res[2 * b]["out"])
        o1 = np.asarray(res[2 * b + 1]["out"])
        out[b, 0:4096] = o0[0:4096]
        out[b, 4096:] = o1[128:]
    return out
```
